# Optimizing a Trainium2 kernel written in Bass

```python
import math
import jax
import jax.numpy as jnp
from jax import lax
import numpy as np

D_MODEL = 4096
BATCH = 4
SEQ = 2048
DEPTH = 2

N_META = 16
GRID_W = 64
Q_BLOCK = 128
EPS = 1e-6

N_BRANCHES = 4
BRANCH_W = D_MODEL // N_BRANCHES
MIX_W = N_BRANCHES * BRANCH_W
HEAD_DIM = 128

A_HEADS = BRANCH_W // HEAD_DIM
A_KV_HEADS = max(1, A_HEADS // 4)
A_GROUPS = A_HEADS // A_KV_HEADS
AXIAL_THETA = 10000.0

HYENA_ORDER = 2
HYENA_CH = BRANCH_W
HYENA_EMB_DIM = 33
HYENA_FFN = 64
HYENA_SHORT = 3
HYENA_FAST = 0.3
HYENA_SLOW = 1.5
HYENA_TARGET = 1e-2
HYENA_SHIFT = 0.05

POOL_WINDOWS = (2, 4, 8, 16)
POOL_GROUPS = len(POOL_WINDOWS)
POOL_GROUP_W = BRANCH_W // POOL_GROUPS

D_HEAD_QK = HEAD_DIM
D_HEADS = BRANCH_W // (2 * D_HEAD_QK)
D_HEAD_V = 2 * D_HEAD_QK
ROPE_THETA = 500000.0
ROPE_DIMS = D_HEAD_QK // 4

A_Q_W = A_HEADS * HEAD_DIM
A_KV_W = A_KV_HEADS * HEAD_DIM
B_W = (HYENA_ORDER + 1) * HYENA_CH
C_W = BRANCH_W
D_QK_W = D_HEADS * 2 * D_HEAD_QK
D_V_W = D_HEADS * D_HEAD_V
GATE_W = MIX_W
SPLITS = (A_Q_W, A_KV_W, A_KV_W, B_W, C_W, D_QK_W, D_QK_W, D_V_W, GATE_W)
IN_COLS = A_Q_W + 2 * A_KV_W + B_W + C_W + 2 * D_QK_W + D_V_W + GATE_W

kernel_name = 'hymba_style_parallel_hybrid_encoder'


def split_points():
    pts, acc = [], 0
    for s in SPLITS[:-1]:
        acc += s
        pts.append(acc)
    return pts


def rms_norm(x, w):
    xf = x.astype(jnp.float32)
    y = xf * lax.rsqrt(jnp.mean(xf * xf, axis=-1, keepdims=True) + EPS)
    return (y * w.astype(jnp.float32)).astype(x.dtype)


def inv_freq(dim, theta):
    return theta ** (-jnp.arange(0, dim, 2, dtype=jnp.float32) / dim)


def rope_rotate(x, ang):
    m = ang.shape[-1]
    shape = (1, ang.shape[0]) + (1,) * (x.ndim - 3) + (m,)
    c = jnp.cos(ang).reshape(shape)
    s = jnp.sin(ang).reshape(shape)
    xf = x.astype(jnp.float32)
    x1, x2 = xf[..., :m], xf[..., m:]
    return jnp.concatenate([x1 * c - x2 * s, x2 * c + x1 * s], axis=-1).astype(x.dtype)


def sweep_query_blocks(block_fn, q):
    q_meta, q_real = q[..., :N_META, :], q[..., N_META:, :]
    nb = q_real.shape[-2] // Q_BLOCK
    qb = q_real.reshape(q_real.shape[:-2] + (nb, Q_BLOCK, q_real.shape[-1]))
    qb = jnp.moveaxis(qb, -3, 0)
    ob = jnp.moveaxis(lax.map(block_fn, qb), 0, -3)
    ob = ob.reshape(ob.shape[:-3] + (nb * Q_BLOCK, ob.shape[-1]))
    return jnp.concatenate([block_fn(q_meta), ob], axis=-2)


def gqa_axial_mixer(aq, ak, av, q_norm, k_norm, ang_row, ang_col):
    B, L, _ = aq.shape
    q = rms_norm(aq.reshape(B, L, A_HEADS, HEAD_DIM), q_norm)
    k = rms_norm(ak.reshape(B, L, A_KV_HEADS, HEAD_DIM), k_norm)
    v = av.reshape(B, L, A_KV_HEADS, HEAD_DIM)
    half = HEAD_DIM // 2

    def axial(t):
        return jnp.concatenate([rope_rotate(t[..., :half], ang_row),
                                rope_rotate(t[..., half:], ang_col)], axis=-1)

    q, k = axial(q), axial(k)
    q = q.reshape(B, L, A_KV_HEADS, A_GROUPS, HEAD_DIM).transpose(0, 2, 3, 1, 4)
    k = k.transpose(0, 2, 1, 3)
    v = v.transpose(0, 2, 1, 3)
    scale = HEAD_DIM ** -0.5

    def block(qb):
        s = jnp.einsum('bkgqd,bksd->bkgqs', qb, k).astype(jnp.float32) * scale
        p = jax.nn.softmax(s, axis=-1).astype(v.dtype)
        return jnp.einsum('bkgqs,bksd->bkgqd', p, v)

    o = sweep_query_blocks(block, q)
    return o.transpose(0, 3, 1, 2, 4).reshape(B, L, A_HEADS * HEAD_DIM)


def short_conv3(u, w):
    L = u.shape[1]
    up = jnp.pad(u, ((0, 0), (1, 1), (0, 0)))
    return up[:, :L] * w[0] + up[:, 1:L + 1] * w[1] + up[:, 2:L + 2] * w[2]


def hyena_filter_spectra(L, w1, b1, w2, b2, w3, freq):
    t = jnp.linspace(0.0, 1.0, L, dtype=jnp.float32)[:, None]
    bands = (HYENA_EMB_DIM - 1) // 2
    fb = jnp.linspace(1e-4, bands - 1, bands, dtype=jnp.float32)[None, :]
    wpos = 2.0 * math.pi * jnp.arange(L, dtype=jnp.float32)[:, None] / L
    z = jnp.concatenate([t, jnp.cos(fb * wpos), -jnp.sin(fb * wpos)], axis=-1)
    fr = freq.astype(jnp.float32)
    hdn = jnp.sin(fr * (z @ w1.astype(jnp.float32) + b1.astype(jnp.float32)))
    hdn = jnp.sin(fr * (hdn @ w2.astype(jnp.float32) + b2.astype(jnp.float32)))
    h = (hdn @ w3.astype(jnp.float32)).reshape(L, HYENA_ORDER, 2, HYENA_CH)
    deltas = jnp.abs(jnp.linspace(math.log(HYENA_TARGET) / HYENA_SLOW,
                                  math.log(HYENA_TARGET) / HYENA_FAST, HYENA_CH, dtype=jnp.float32))
    h = h * (jnp.exp(-t * deltas) + HYENA_SHIFT)[:, None, None, :]
    h_two = jnp.concatenate([h[:, :, 0],
                             jnp.zeros((1, HYENA_ORDER, HYENA_CH), jnp.float32),
                             h[1:, :, 1][::-1]], axis=0)
    return jnp.fft.rfft(h_two, axis=0)


def hyena_mixer(bvx, conv_w, w1, b1, w2, b2, w3, freq, skip):
    B, L, _ = bvx.shape
    u = short_conv3(bvx, conv_w)
    v, x1, x2 = jnp.split(u, 3, axis=-1)
    hf = hyena_filter_spectra(L, w1, b1, w2, b2, w3, freq)
    z = v
    for o, gate in enumerate((x1, x2)):
        zf = z.astype(jnp.float32)
        conv = jnp.fft.irfft(jnp.fft.rfft(zf, n=2 * L, axis=1) * hf[None, :, o], n=2 * L, axis=1)[:, :L]
        z = (gate.astype(jnp.float32) * (conv + skip[o].astype(jnp.float32) * zf)).astype(bvx.dtype)
    return z


def pool_mixer(p, group_w, scale):
    B, L, C = p.shape
    pf = p.astype(jnp.float32)
    cs = jnp.concatenate([jnp.zeros((B, 1, C), jnp.float32), jnp.cumsum(pf, axis=1)], axis=1)
    t = jnp.arange(L)
    outs = []
    for g, w in enumerate(POOL_WINDOWS):
        lo = jnp.clip(t - (w - 1) // 2, 0, L)
        hi = jnp.clip(t + w // 2 + 1, 0, L)
        sl = slice(g * POOL_GROUP_W, (g + 1) * POOL_GROUP_W)
        csg = cs[..., sl]
        cnt = (hi - lo).astype(jnp.float32)[None, :, None]
        outs.append((csg[:, hi] - csg[:, lo]) / cnt - pf[..., sl])
    d = jnp.stack(outs, axis=2).astype(p.dtype)
    y = jnp.einsum('blgc,gce->blge', d, group_w)
    return y.reshape(B, L, C) * scale


def diff_attention_mixer(dq, dk, dv, q_norm, k_norm, lam_vec, out_norm, lam_init, ang):
    B, L, _ = dq.shape

    def prep(t, g):
        t = rms_norm(t.reshape(B, L, D_HEADS, 2, D_HEAD_QK), g)
        t = jnp.concatenate([rope_rotate(t[..., :ROPE_DIMS], ang), t[..., ROPE_DIMS:]], axis=-1)
        return t.reshape(B, L, D_HEADS, 2 * D_HEAD_QK).transpose(0, 2, 1, 3)

    q, k = prep(dq, q_norm), prep(dk, k_norm)
    v = dv.reshape(B, L, D_HEADS, D_HEAD_V).transpose(0, 2, 1, 3)
    lf = lam_vec.astype(jnp.float32)
    lam = jnp.exp(jnp.sum(lf[0] * lf[1])) - jnp.exp(jnp.sum(lf[2] * lf[3])) + lam_init
    k1, k2 = k[..., :D_HEAD_QK], k[..., D_HEAD_QK:]
    scale = D_HEAD_QK ** -0.5

    def block(qb):
        s1 = jnp.einsum('bhqd,bhsd->bhqs', qb[..., :D_HEAD_QK], k1).astype(jnp.float32) * scale
        s2 = jnp.einsum('bhqd,bhsd->bhqs', qb[..., D_HEAD_QK:], k2).astype(jnp.float32) * scale
        a = jax.nn.softmax(s1, axis=-1) - lam * jax.nn.softmax(s2, axis=-1)
        return jnp.einsum('bhqs,bhsd->bhqd', a.astype(v.dtype), v)

    o = sweep_query_blocks(block, q)
    o = rms_norm(o, out_norm) * (1.0 - lam_init)
    return o.transpose(0, 2, 1, 3).reshape(B, L, D_HEADS * D_HEAD_V)


def setup_inputs(seed: int = 0) -> dict:
    key = jax.random.key(seed)
    ks = jax.random.split(key, 24)

    def nrm(k, shape, s):
        return jax.random.normal(k, shape, jnp.float32) * s

    def gain(k, shape):
        return 1.0 + 0.02 * jax.random.normal(k, shape, jnp.float32)

    return {
        'x': nrm(ks[0], (BATCH, SEQ, D_MODEL), 1.0),
        'meta_tokens': nrm(ks[1], (N_META, D_MODEL), 1.0),
        'norm_w': gain(ks[2], (DEPTH, D_MODEL)),
        'w_in': nrm(ks[3], (DEPTH, D_MODEL, IN_COLS), D_MODEL ** -0.5),
        'w_out': nrm(ks[4], (DEPTH, MIX_W, D_MODEL), MIX_W ** -0.5),
        'a_q_norm': gain(ks[5], (DEPTH, HEAD_DIM)),
        'a_k_norm': gain(ks[6], (DEPTH, HEAD_DIM)),
        'a_out_norm': gain(ks[7], (DEPTH, BRANCH_W)),
        'b_short_conv': nrm(ks[8], (DEPTH, HYENA_SHORT, B_W), HYENA_SHORT ** -0.5),
        'b_filt_w1': nrm(ks[9], (DEPTH, HYENA_EMB_DIM, HYENA_FFN), HYENA_EMB_DIM ** -0.5),
        'b_filt_b1': nrm(ks[10], (DEPTH, HYENA_FFN), 0.1),
        'b_filt_w2': nrm(ks[11], (DEPTH, HYENA_FFN, HYENA_FFN), HYENA_FFN ** -0.5),
        'b_filt_b2': nrm(ks[12], (DEPTH, HYENA_FFN), 0.1),
        'b_filt_w3': nrm(ks[13], (DEPTH, HYENA_FFN, HYENA_ORDER * 2 * HYENA_CH), 0.1 * HYENA_FFN ** -0.5),
        'b_sin_freq': gain(ks[14], (DEPTH, HYENA_FFN)),
        'b_skip': nrm(ks[15], (DEPTH, HYENA_ORDER, HYENA_CH), 0.5),
        'b_out_norm': gain(ks[16], (DEPTH, BRANCH_W)),
        'c_group_w': nrm(ks[17], (DEPTH, POOL_GROUPS, POOL_GROUP_W, POOL_GROUP_W), POOL_GROUP_W ** -0.5),
        'c_scale': gain(ks[18], (DEPTH, BRANCH_W)),
        'd_q_norm': gain(ks[19], (DEPTH, D_HEAD_QK)),
        'd_k_norm': gain(ks[20], (DEPTH, D_HEAD_QK)),
        'd_lambda': nrm(ks[21], (DEPTH, 4, D_HEAD_QK), 0.1),
        'd_out_norm': gain(ks[22], (DEPTH, D_HEAD_V)),
    }


def reference(x, meta_tokens, norm_w, w_in, w_out, a_q_norm, a_k_norm, a_out_norm,
              b_short_conv, b_filt_w1, b_filt_b1, b_filt_w2, b_filt_b2, b_filt_w3,
              b_sin_freq, b_skip, b_out_norm, c_group_w, c_scale,
              d_q_norm, d_k_norm, d_lambda, d_out_norm):
    B = x.shape[0]
    meta = jnp.broadcast_to(meta_tokens.astype(x.dtype)[None], (B, N_META, x.shape[-1]))
    h = jnp.concatenate([meta, x], axis=1)
    L = h.shape[1]
    n_real = L - N_META
    rows = n_real // GRID_W

    zeros_meta = jnp.zeros((N_META,), jnp.float32)
    row = jnp.concatenate([zeros_meta, jnp.repeat(jnp.arange(rows, dtype=jnp.float32), GRID_W)])
    col = jnp.concatenate([zeros_meta, jnp.tile(jnp.arange(GRID_W, dtype=jnp.float32), rows)])
    ax_freq = inv_freq(HEAD_DIM // 2, AXIAL_THETA)
    ang_row = row[:, None] * ax_freq[None, :]
    ang_col = col[:, None] * ax_freq[None, :]
    ang_1d = jnp.arange(L, dtype=jnp.float32)[:, None] * inv_freq(ROPE_DIMS, ROPE_THETA)[None, :]
    pts = split_points()

    for l in range(DEPTH):
        lam_init = 0.8 - 0.6 * math.exp(-0.3 * l)
        u = rms_norm(h, norm_w[l])
        proj = u @ w_in[l]
        aq, ak, av, bvx, cp, dq, dk, dv, gate = jnp.split(proj, pts, axis=-1)
        ya = rms_norm(gqa_axial_mixer(aq, ak, av, a_q_norm[l], a_k_norm[l], ang_row, ang_col), a_out_norm[l])
        yb = rms_norm(hyena_mixer(bvx, b_short_conv[l], b_filt_w1[l], b_filt_b1[l], b_filt_w2[l],
                                  b_filt_b2[l], b_filt_w3[l], b_sin_freq[l], b_skip[l]), b_out_norm[l])
        yc = pool_mixer(cp, c_group_w[l], c_scale[l])
        yd = diff_attention_mixer(dq, dk, dv, d_q_norm[l], d_k_norm[l], d_lambda[l], d_out_norm[l],
                                  lam_init, ang_1d)
        y = jnp.concatenate([ya, yb, yc, yd], axis=-1) * jax.nn.silu(gate)
        h = h + y @ w_out[l]

    return h[:, N_META:]
```

```python
import contextlib
import math
import numpy as np
import ml_dtypes
import concourse.bass as bass
import concourse.mybir as mybir
from concourse.bass_utils import run_bass_kernel_spmd

F32 = mybir.dt.float32
BF16 = mybir.dt.bfloat16
AF = mybir.ActivationFunctionType
ALU = mybir.AluOpType
NPBF = ml_dtypes.bfloat16

DM = 4096
NMETA = 16
SEQ = 2048
L = NMETA + SEQ
HALF = L // 2
DEPTH = 2
EPS = 1e-6
NFFT = 4224
NF = 2176
NFM = 5248
NTM = 1152
NPRM = 104

TT = [(i * 128, 128) for i in range(16)] + [(2048, 16)]
TCH = [(i * 512, 512) for i in range(4)] + [(2048, 16)]
HCH = [(0, 512), (512, 512), (1024, 8)]
HTT = [(i * 128, 128) for i in range(8)] + [(1024, 8)]
FG = [(0, 4), (4, 4), (8, 4), (12, 4), (16, 1)]


class Buf:
    __slots__ = ("name", "last_w", "readers", "dsem", "multi", "writers")

    def __init__(self, name, multi=False):
        self.name = name
        self.last_w = None
        self.readers = {}
        self.dsem = None
        self.multi = multi
        self.writers = {}


class Prog:
    ENGS = ("pe", "act", "dve", "pool", "sp")

    def __init__(self, nc, stack):
        self.nc = nc
        self.stack = stack
        self.ops = {e: [] for e in self.ENGS}
        self.sems = {}
        self.semcount = {}
        self.known = {e: {} for e in self.ENGS}
        for e in ("pe", "act", "dve", "pool"):
            self._newsem("E_" + e)
        self.ndsem = 0
        self.free_dsems = {"sp": [], "pool": []}
        self.sem_kind = {}

    def _newsem(self, key):
        s = self.stack.enter_context(self.nc.semaphore(key))
        self.sems[key] = s
        self.semcount[key] = 0
        return key

    def buf(self, name, multi=False):
        return Buf(name, multi)

    def dbuf(self, name, multi=False):
        b = Buf(name, multi)
        b.dsem = "lazy"
        return b

    def _get_dsem(self, b, eng):
        if b.dsem == "lazy":
            if self.free_dsems[eng]:
                b.dsem = self.free_dsems[eng].pop()
            else:
                self.ndsem += 1
                b.dsem = self._newsem("D%d" % self.ndsem)
                self.sem_kind[b.dsem] = eng
        assert self.sem_kind[b.dsem] == eng, "DMA semaphore of %s used from two queue kinds" % b.name
        return b.dsem

    def release(self, bufs):
        for b in bufs:
            if b.dsem is not None and b.dsem != "lazy":
                self.free_dsems[self.sem_kind[b.dsem]].append(b.dsem)
            b.dsem = None

    def _wait(self, eng, waits, ref):
        if ref is None:
            return
        key, val, src = ref
        if src == "pe" and eng == "pe":
            return
        if self.known[eng].get(key, 0) >= val:
            return
        self.known[eng][key] = val
        waits.append((key, val))

    def op(self, eng, fn, reads=(), writes=(), dsem=None):
        waits = []
        for b in reads:
            if b.multi:
                for r in b.writers.values():
                    self._wait(eng, waits, r)
            else:
                self._wait(eng, waits, b.last_w)
        for b in writes:
            if not b.multi:
                self._wait(eng, waits, b.last_w)
            for r in b.readers.values():
                self._wait(eng, waits, r)
        if dsem is not None:
            if self.semcount[dsem] > 0:
                self._wait(eng, waits, (dsem, self.semcount[dsem], "dma"))
            self.semcount[dsem] += 16
            ref = (dsem, self.semcount[dsem], "dma")
            inc = (dsem, 16)
        else:
            key = "E_" + eng
            self.semcount[key] += 1
            ref = (key, self.semcount[key], eng)
            inc = (key, 1)
        for b in reads:
            b.readers[ref[0]] = ref
        for b in writes:
            if b.multi:
                b.writers[ref[0]] = ref
            else:
                b.last_w = ref
                b.readers = {}
        self.ops[eng].append((waits, fn, inc))
        return ref

    def dma(self, out_ap, in_ap, reads=(), writes=(), sem_buf=None, eng="sp", **kw):
        sb = sem_buf
        if sb is None:
            for b in list(writes) + list(reads):
                if b.dsem is not None:
                    sb = b
                    break
        assert sb is not None and sb.dsem is not None, "dma needs a dsem buffer"
        return self.op(eng, lambda e: e.dma_start(out=out_ap, in_=in_ap, **kw),
                       reads=reads, writes=writes, dsem=self._get_dsem(sb, eng))

    def mm(self, out_ap, lhsT, rhs, start, stop, reads=(), writes=()):
        return self.op("pe", lambda e: e.matmul(out_ap, lhsT, rhs, start=start, stop=stop),
                       reads=reads, writes=writes)

    def finish(self, eng="sp"):
        waits = []
        for key, cnt in self.semcount.items():
            if key.startswith("D") and cnt > 0:
                self._wait(eng, waits, (key, cnt, "dma"))
        self.ops[eng].append((waits, None, None))

    def emit(self):
        nc = self.nc
        with nc.Block() as block:
            def run(engname):
                def f(e):
                    for waits, fn, inc in self.ops[engname]:
                        for key, val in waits:
                            e.wait_ge(self.sems[key], val)
                        if fn is None:
                            continue
                        ins = fn(e)
                        ins.then_inc(self.sems[inc[0]], inc[1])
                return f
            block.tensor(run("pe"))
            block.scalar(run("act"))
            block.vector(run("dve"))
            block.gpsimd(run("pool"))
            block.sync(run("sp"))
        self.ops = {e: [] for e in self.ENGS}


class Ctx:
    def __init__(self, nc, P, roles):
        self.nc = nc
        self.P = P
        self.roles = roles
        self.dram = {}
        self.dbufs = {}

    def dt(self, name, shape, dtype):
        if name not in self.dram:
            role = self.roles.get(name, "scratch")
            kind = {"in": "ExternalInput", "out": "ExternalOutput", "scratch": "Internal"}[role]
            self.dram[name] = self.nc.dram_tensor(name, list(shape), dtype, kind=kind).ap()
            self.dbufs[name] = self.P.buf("dram_" + name, multi=True)
        return self.dram[name], self.dbufs[name]


class Scope:
    def __init__(self, C, name):
        self.C = C
        self.name = name
        self.st = contextlib.ExitStack()
        self.n = 0
        self.bufs = []

    def __enter__(self):
        self.st.__enter__()
        return self

    def __exit__(self, *a):
        self.C.P.emit()
        self.C.P.release(self.bufs)
        return self.st.__exit__(*a)

    def sb(self, shape, dtype, tag="t"):
        self.n += 1
        return self.st.enter_context(self.C.nc.sbuf_tensor("%s_%s%d" % (self.name, tag, self.n), list(shape), dtype))

    def psum(self, tag="ps"):
        self.n += 1
        return self.st.enter_context(self.C.nc.psum_tensor("%s_%s%d" % (self.name, tag, self.n), [128, 512], F32))

    def buf(self, name="b", multi=False):
        return self.C.P.buf(name, multi)

    def dbuf(self, name="d", multi=False):
        b = self.C.P.dbuf(name, multi)
        self.bufs.append(b)
        return b


_CONST_CACHE = {}


def _bf(a):
    return np.ascontiguousarray(a.astype(np.float32)).astype(NPBF)


def host_consts():
    if "c" in _CONST_CACHE:
        return _CONST_CACHE["c"]
    c = {}
    t = np.arange(L)
    row = np.where(t < NMETA, 0, (t - NMETA) // 64).astype(np.float32)
    col = np.where(t < NMETA, 0, (t - NMETA) % 64).astype(np.float32)
    fA = (np.float32(10000.0) ** (-np.arange(0, 64, 2, dtype=np.float32) / np.float32(64))).astype(np.float32)
    rotA = np.zeros((128, 2, L), np.float32)
    for d in range(128):
        j = d % 32
        ang = (row if d < 64 else col) * fA[j]
        rotA[d, 0] = np.cos(ang.astype(np.float32))
        rotA[d, 1] = np.sin(ang.astype(np.float32))
    fD = (np.float32(500000.0) ** (-np.arange(0, 32, 2, dtype=np.float32) / np.float32(32))).astype(np.float32)
    rotD = np.zeros((128, 2, L), np.float32)
    rotD[:, 0, :] = 1.0
    for d in range(32):
        j = d % 16
        ang = (t.astype(np.float32) * fD[j]).astype(np.float32)
        rotD[d, 0] = np.cos(ang)
        rotD[d, 1] = np.sin(ang)
    c["rotA"] = rotA
    c["rotD"] = rotD
    Rm = np.zeros((128, 2, 128), np.float32)
    for d in range(128):
        blk = (d // 64) * 64
        dd = d - blk
        if dd < 32:
            Rm[blk + dd + 32, 0, d] = -1.0
        else:
            Rm[blk + dd - 32, 0, d] = 1.0
    for d in range(32):
        if d < 16:
            Rm[d + 16, 1, d] = -1.0
        else:
            Rm[d - 16, 1, d] = 1.0
    c["Rm"] = _bf(Rm)
    c["ident"] = _bf(np.eye(128, dtype=np.float32))
    a = np.arange(NF, dtype=np.int64)
    ab = (a[:, None] * a[None, :]) % NFFT
    ang = ab.astype(np.float64) * (2.0 * np.pi / NFFT)
    c["Tc"] = _bf(np.cos(ang))
    c["Ts"] = _bf(np.sin(ang))
    f = np.arange(NF)
    wf = np.where(f > NFFT // 2, 0.0, np.where((f == 0) | (f == NFFT // 2), 1.0 / NFFT, 2.0 / NFFT))
    c["wf"] = np.ascontiguousarray(wf.reshape(17, 128).T.astype(np.float32))
    tt = np.linspace(0.0, 1.0, L, dtype=np.float32)[:, None]
    bands = 16
    fb = np.linspace(1e-4, bands - 1, bands, dtype=np.float32)[None, :]
    wpos = (np.float32(2.0 * math.pi) * np.arange(L, dtype=np.float32)[:, None] / np.float32(L)).astype(np.float32)
    z = np.concatenate([tt, np.cos(fb * wpos), -np.sin(fb * wpos)], axis=-1).astype(np.float32)
    c["zfT"] = np.ascontiguousarray(z.T)
    deltas = np.abs(np.linspace(math.log(1e-2) / 1.5, math.log(1e-2) / 0.3, 1024, dtype=np.float32))
    decay = (np.exp(-tt * deltas[None, :]) + np.float32(0.05)).astype(np.float32)
    c["decay"] = [np.ascontiguousarray(decay[:, r * 512:(r + 1) * 512]) for r in range(2)]
    pools = []
    for w in (2, 4, 8, 16):
        lo = np.clip(t - (w - 1) // 2, 0, L)
        hi = np.clip(t + w // 2 + 1, 0, L)
        Pm = np.zeros((L, L), np.float32)
        for ti in range(L):
            Pm[lo[ti]:hi[ti], ti] = 1.0 / float(hi[ti] - lo[ti])
            Pm[ti, ti] -= 1.0
        blk = np.zeros((5, 128, 6, 512), np.float32)
        for j, (t0, tw) in enumerate(TCH):
            for k in range(6):
                st = 4 * j - 1 + k
                if st < 0 or st > 16:
                    continue
                s0, sr = TT[st]
                blk[j, :sr, k, :tw] = Pm[s0:s0 + sr, t0:t0 + tw]
        pools.append(_bf(blk))
    c["pool"] = [np.ascontiguousarray(np.stack([pools[2 * r], pools[2 * r + 1]])) for r in range(2)]
    _CONST_CACHE["c"] = c
    return c


def pack_layer_params(inp, l, r):
    prm = np.zeros((128, NPRM), np.float32)
    prm[:, 0:32] = inp["norm_w"][l].reshape(32, 128).T

    def permA(g):
        out = np.empty_like(g)
        for d in range(128):
            blk = (d // 64) * 64
            dd = d - blk
            out[d] = g[blk + (dd + 32 if dd < 32 else dd - 32)]
        return out

    def permD(g):
        out = g.copy()
        for d in range(32):
            out[d] = g[d + 16 if d < 16 else d - 16]
        return out

    prm[:, 32] = inp["a_q_norm"][l]
    prm[:, 33] = permA(inp["a_q_norm"][l])
    prm[:, 34] = inp["a_k_norm"][l]
    prm[:, 35] = permA(inp["a_k_norm"][l])
    prm[:, 36] = inp["d_q_norm"][l]
    prm[:, 37] = permD(inp["d_q_norm"][l])
    prm[:, 38] = inp["d_k_norm"][l]
    prm[:, 39] = permD(inp["d_k_norm"][l])
    prm[:, 40:44] = inp["a_out_norm"][l][r * 512:(r + 1) * 512].reshape(4, 128).T
    prm[:, 44:48] = inp["b_out_norm"][l][r * 512:(r + 1) * 512].reshape(4, 128).T
    prm[:, 48:52] = inp["c_scale"][l][r * 512:(r + 1) * 512].reshape(4, 128).T
    prm[:, 52:54] = inp["d_out_norm"][l].reshape(2, 128).T
    for tap in range(3):
        for sec in range(3):
            v = inp["b_short_conv"][l][tap, sec * 1024 + r * 512: sec * 1024 + (r + 1) * 512]
            prm[:, 54 + tap * 12 + sec * 4: 54 + tap * 12 + sec * 4 + 4] = v.reshape(4, 128).T
    for o in range(2):
        prm[:, 90 + o * 4: 94 + o * 4] = inp["b_skip"][l][o, r * 512:(r + 1) * 512].reshape(4, 128).T
    prm[0:64, 98] = inp["b_filt_b1"][l]
    prm[0:64, 99] = inp["b_filt_b2"][l]
    prm[0:64, 100] = inp["b_sin_freq"][l]
    return prm


def fm_cols(r):
    cols = []
    cols += list(range(0 + r * 512, 0 + r * 512 + 512))
    cols += list(range(1024 + r * 128, 1024 + r * 128 + 128))
    for sec in range(3):
        cols += list(range(1536 + sec * 1024 + r * 512, 1536 + sec * 1024 + r * 512 + 512))
    cols += list(range(5632 + r * 512, 5632 + r * 512 + 512))
    cols += list(range(6656 + r * 512, 6656 + r * 512 + 512))
    for br in range(4):
        cols += list(range(8704 + br * 1024 + r * 512, 8704 + br * 1024 + r * 512 + 512))
    assert len(cols) == NFM
    return np.array(cols)


def tm_cols(r):
    cols = []
    cols += list(range(4608 + r * 512, 4608 + r * 512 + 512))
    cols += list(range(7680 + r * 512, 7680 + r * 512 + 512))
    cols += list(range(1280 + r * 128, 1280 + r * 128 + 128))
    assert len(cols) == NTM
    return np.array(cols)


def y_rows(r):
    rows = []
    for br in range(4):
        rows += list(range(br * 1024 + r * 512, br * 1024 + r * 512 + 512))
    return np.array(rows)


def load_consts(C, S, names):
    P = C.P
    cst = S.sb([128, 8], F32, "cst")
    Bc = S.buf("cst")
    P.op("dve", lambda e: e.memset(cst[:, 0:1], EPS), writes=[Bc])
    P.op("dve", lambda e: e.memset(cst[:, 1:2], 0.0), writes=[Bc])
    P.op("dve", lambda e: e.memset(cst[:, 2:3], 1.0), writes=[Bc])
    return cst, Bc


def phase_p0(C, lname):
    P, nc = C.P, C.nc
    hT, BhT = C.dt(lname + "hT", [DM, HALF], F32)
    uT, BuT = C.dt(lname + "uT_half", [DM, HALF], BF16)
    with Scope(C, lname + "p0") as S:
        cst, Bc = load_consts(C, S, None)
        acc = S.sb([128, HALF], F32); Bacc = S.buf()
        ht = [S.sb([128, HALF], F32) for _ in range(2)]; Bht = [S.dbuf() for _ in range(2)]
        sq = [S.sb([128, HALF], F32) for _ in range(2)]; Bsq = [S.buf() for _ in range(2)]
        ub = [S.sb([128, HALF], BF16) for _ in range(2)]; Bub = [S.dbuf() for _ in range(2)]
        rstd = S.sb([128, HALF], F32); Brs = S.buf()
        tmp = S.sb([128, HALF], F32); Btmp = S.buf()
        onesf = S.sb([128, 128], F32); Bones = S.buf()
        ps = [S.psum() for _ in range(3)]; Bps = [S.buf() for _ in range(3)]
        P.op("pool", lambda e: e.memset(acc[:], 0.0), writes=[Bacc])
        P.op("pool", lambda e: e.memset(onesf[:], 1.0), writes=[Bones])
        for kt in range(32):
            s = kt % 2
            P.dma(ht[s][:], hT[kt * 128:(kt + 1) * 128, :], reads=[BhT], writes=[Bht[s]])
            P.op("act", lambda e, s=s: e.activation(out=sq[s][:], in_=ht[s][:], func=AF.Square),
                 reads=[Bht[s]], writes=[Bsq[s]])
            P.op("pool", lambda e, s=s: e.tensor_tensor(acc[:], acc[:], sq[s][:], ALU.add),
                 reads=[Bsq[s]], writes=[Bacc])
        for ci, (c0, w) in enumerate(HCH):
            P.mm(ps[ci][:, :w], onesf[:], acc[:, c0:c0 + w], True, True, reads=[Bones, Bacc], writes=[Bps[ci]])
            P.op("act", lambda e, ci=ci, c0=c0, w=w: e.activation(out=tmp[:, c0:c0 + w], in_=ps[ci][:, :w], func=AF.Sqrt,
                                                              scale=1.0 / DM, bias=cst[:, 0:1]),
                 reads=[Bps[ci], Bc], writes=[Btmp])
        P.op("dve", lambda e: e.reciprocal(rstd[:], tmp[:]), reads=[Btmp], writes=[Brs])
        for kt in range(32):
            s = kt % 2
            P.dma(ht[s][:], hT[kt * 128:(kt + 1) * 128, :], reads=[BhT], writes=[Bht[s]])
            P.op("dve", lambda e, s=s: e.tensor_tensor(ub[s][:], ht[s][:], rstd[:], ALU.mult),
                 reads=[Bht[s], Brs], writes=[Bub[s]])
            P.dma(uT[kt * 128:(kt + 1) * 128, :], ub[s][:], reads=[Bub[s]], writes=[BuT], eng="pool")
        P.finish()


def phase_p1(C, lname):
    P, nc = C.P, C.nc
    uTg, BuTg = C.dt(lname + "uT_g", [2, DM, HALF], BF16)
    wfm, Bwfm = C.dt(lname + "w_fm", [DM, NFM], F32)
    wtm, Bwtm = C.dt(lname + "w_tm", [DM, NTM], F32)
    prm_d, Bprm_d = C.dt(lname + "prm", [128, NPRM], F32)
    projT, BprojT = C.dt(lname + "projT", [NFM, L], BF16)
    ptok, Bptok = C.dt(lname + "ptok", [L, NTM], BF16)
    groups = [("fm", i * 512, 512) for i in range(10)] + [("fm", 5120, 128)] + \
             [("tm", 0, 512), ("tm", 512, 512), ("tm", 1024, 128)]
    with Scope(C, lname + "p1") as S:
        prm = S.sb([128, NPRM], F32); Bprm = S.dbuf()
        P.dma(prm[:], prm_d[:, :], reads=[Bprm_d], writes=[Bprm])
        uT = S.sb([128, 32, HALF], BF16); BuT = [S.dbuf() for _ in range(4)]
        wst = [S.sb([128, 8, 512], F32) for _ in range(2)]; Bwst = [S.dbuf() for _ in range(2)]
        wb = [S.sb([128, 32, 512], BF16) for _ in range(2)]
        Bwb = [[S.buf(multi=True) for _ in range(4)] for _ in range(2)]
        ot = [S.sb([128, HALF], BF16) for _ in range(2)]; Bot = [S.dbuf() for _ in range(2)]
        otm = [S.sb([128, 512], BF16) for _ in range(2)]; Botm = [S.dbuf() for _ in range(2)]
        psA = [[S.psum() for _ in range(3)] for _ in range(2)]
        BpsA = [[S.buf() for _ in range(3)] for _ in range(2)]
        psT = [S.psum() for _ in range(2)]; BpsT = [S.buf() for _ in range(2)]
        cnt = {"piece": 0, "cast": 0, "fm": 0, "tm": 0, "ev": 0}

        def load_group(gi, g):
            kind, c0, w = g
            src, Bsrc = (wfm, Bwfm) if kind == "fm" else (wtm, Bwtm)
            gs = gi % 2
            for pc in range(4):
                s = cnt["piece"] % 2
                cnt["piece"] += 1
                P.dma(wst[s][:, :, :w],
                      src[pc * 1024:(pc + 1) * 1024, c0:c0 + w].rearrange("(k p) c -> p k c", p=128),
                      reads=[Bsrc], writes=[Bwst[s]])
                for k in range(8):
                    kt = pc * 8 + k
                    if cnt["cast"] % 2 == 0:
                        P.op("act", lambda e, s=s, k=k, kt=kt, gs=gs, w=w: e.activation(
                            out=wb[gs][:, kt, :w], in_=wst[s][:, k, :w], func=AF.Copy, scale=prm[:, kt:kt + 1]),
                            reads=[Bwst[s], Bprm], writes=[Bwb[gs][pc]])
                    else:
                        P.op("dve", lambda e, s=s, k=k, kt=kt, gs=gs, w=w: e.tensor_scalar(
                            wb[gs][:, kt, :w], wst[s][:, k, :w], prm[:, kt:kt + 1], None, ALU.mult),
                            reads=[Bwst[s], Bprm], writes=[Bwb[gs][pc]])
                    cnt["cast"] += 1

        def evac(out_ap, in_ap, reads, writes):
            if cnt["ev"] % 2 == 0:
                P.op("act", lambda e: e.activation(out=out_ap, in_=in_ap, func=AF.Copy), reads=reads, writes=writes)
            else:
                P.op("dve", lambda e: e.tensor_copy(out=out_ap, in_=in_ap), reads=reads, writes=writes)
            cnt["ev"] += 1

        def compute_group(gi, g, pas):
            kind, c0, w = g
            gs = gi % 2
            if kind == "fm":
                for j in range(w // 128):
                    ss = cnt["fm"] % 2
                    cnt["fm"] += 1
                    for kt in range(32):
                        for ci, (t0, tw) in enumerate(HCH):
                            P.mm(psA[ss][ci][:, :tw], wb[gs][:, kt, j * 128:(j + 1) * 128], uT[:, kt, t0:t0 + tw],
                                 kt == 0, kt == 31, reads=[Bwb[gs][kt // 8], BuT[kt // 8]], writes=[BpsA[ss][ci]])
                    for ci, (t0, tw) in enumerate(HCH):
                        evac(ot[ss][:, t0:t0 + tw], psA[ss][ci][:, :tw], [BpsA[ss][ci]], [Bot[ss]])
                    P.dma(projT[c0 + j * 128:c0 + (j + 1) * 128, pas * HALF:(pas + 1) * HALF], ot[ss][:],
                          reads=[Bot[ss]], writes=[BprojT], eng="pool")
            else:
                for (t0, tr) in HTT:
                    ss = cnt["tm"] % 2
                    cnt["tm"] += 1
                    for kt in range(32):
                        P.mm(psT[ss][:tr, :w], uT[:, kt, t0:t0 + tr], wb[gs][:, kt, :w], kt == 0, kt == 31,
                             reads=[Bwb[gs][kt // 8], BuT[kt // 8]], writes=[BpsT[ss]])
                    evac(otm[ss][:tr, :w], psT[ss][:tr, :w], [BpsT[ss]], [Botm[ss]])
                    P.dma(ptok[pas * HALF + t0:pas * HALF + t0 + tr, c0:c0 + w], otm[ss][:tr, :w],
                          reads=[Botm[ss]], writes=[Bptok], eng="pool")

        for pas in range(2):
            for q in range(4):
                P.dma(uT[:, q * 8:(q + 1) * 8, :],
                      uTg[pas, q * 1024:(q + 1) * 1024, :].rearrange("(k p) t -> p k t", p=128),
                      reads=[BuTg], writes=[BuT[q]])
            load_group(0, groups[0])
            for gi, g in enumerate(groups):
                if gi + 1 < len(groups):
                    load_group(gi + 1, groups[gi + 1])
                compute_group(gi, g, pas)
        P.finish()


def build_program(phases, roles):
    nc = bass.Bass("TRN2", target_bir_lowering=False)
    with contextlib.ExitStack() as st0:
        P = Prog(nc, st0)
        C = Ctx(nc, P, roles)
        for ph in phases:
            ph(C)
    return nc


def qk_prep(C, S, W, src_ap, Bsrc, dst, Bdst, Cg, Sg, Bcs, ridx):
    P = C.P
    i = W["n"] % 2
    W["n"] += 1
    xr, Bxr = W["xraw"][i], W["Bxraw"][i]
    P.dma(xr[:], src_ap, reads=[Bsrc], writes=[Bxr])
    for (t0, tw) in TCH:
        j = W["m"] % 2
        W["m"] += 1
        sqb, Bsqb = W["sqb"][j], W["Bsqb"][j]
        t1, Bt1 = W["t1"][j], W["Bt1"][j]
        t2, Bt2 = W["t2"][j], W["Bt2"][j]
        sd, Bsd = W["sd"][j], W["Bsd"][j]
        ps1, Bps1 = W["ps1"][j], W["Bps1"][j]
        ps2, Bps2 = W["ps2"][j], W["Bps2"][j]
        P.op("pool", lambda e, t0=t0, tw=tw, sqb=sqb, xr=xr: e.tensor_tensor(sqb[:, :tw], xr[:, t0:t0 + tw], xr[:, t0:t0 + tw], ALU.mult),
             reads=[Bxr], writes=[Bsqb])
        P.mm(ps1[:, :tw], W["onesb"][:], sqb[:, :tw], True, True, reads=[W["Bones"], Bsqb], writes=[Bps1])
        P.mm(ps2[:, :tw], W["Rm"][:, ridx, :], xr[:, t0:t0 + tw], True, True, reads=[W["BRm"], Bxr], writes=[Bps2])
        P.op("dve", lambda e, t0=t0, tw=tw, t1=t1, xr=xr: e.tensor_tensor(t1[:, :tw], xr[:, t0:t0 + tw], Cg[:, t0:t0 + tw], ALU.mult),
             reads=[Bxr, Bcs], writes=[Bt1])
        P.op("dve", lambda e, t0=t0, tw=tw, t2=t2, ps2=ps2: e.tensor_tensor(t2[:, :tw], ps2[:, :tw], Sg[:, t0:t0 + tw], ALU.mult),
             reads=[Bps2, Bcs], writes=[Bt2])
        P.op("pool", lambda e, tw=tw, t1=t1, t2=t2: e.tensor_tensor(t1[:, :tw], t1[:, :tw], t2[:, :tw], ALU.add),
             reads=[Bt2], writes=[Bt1])
        P.op("act", lambda e, tw=tw, sd=sd, ps1=ps1: e.activation(out=sd[:, :tw], in_=ps1[:, :tw], func=AF.Sqrt, scale=1.0 / 128,
                                                            bias=W["cst"][:, 0:1]),
             reads=[Bps1, W["Bc"]], writes=[Bsd])
        P.op("dve", lambda e, tw=tw, sd=sd: e.reciprocal(sd[:, :tw], sd[:, :tw]), reads=[Bsd], writes=[Bsd])
        P.op("dve", lambda e, t0=t0, tw=tw, t1=t1, sd=sd: e.tensor_tensor(dst[:, t0:t0 + tw], t1[:, :tw], sd[:, :tw], ALU.mult),
             reads=[Bt1, Bsd], writes=[Bdst])


def attn_common(C, S, rot_name, ridx, nps):
    P = C.P
    W = {"n": 0, "m": 0}
    W["cst"], W["Bc"] = load_consts(C, S, None)
    rot_d, Brot_d = C.dt(rot_name, [128, 2, L], F32)
    Rm_d, BRm_d = C.dt("Rm", [128, 2, 128], BF16)
    rot = S.sb([128, 2, L], F32); Brot = S.dbuf()
    P.dma(rot[:], rot_d[:, :, :], reads=[Brot_d], writes=[Brot])
    W["rot"], W["Brot"] = rot, Brot
    W["Rm"] = S.sb([128, 2, 128], BF16); W["BRm"] = S.dbuf()
    P.dma(W["Rm"][:], Rm_d[:, :, :], reads=[BRm_d], writes=[W["BRm"]])
    W["onesb"] = S.sb([128, 128], BF16); W["Bones"] = S.buf()
    P.op("pool", lambda e: e.memset(W["onesb"][:], 1.0), writes=[W["Bones"]])
    W["xraw"] = [S.sb([128, L], BF16) for _ in range(2)]; W["Bxraw"] = [S.dbuf() for _ in range(2)]
    W["sqb"] = [S.sb([128, 512], BF16) for _ in range(2)]; W["Bsqb"] = [S.buf() for _ in range(2)]
    W["t1"] = [S.sb([128, 512], F32) for _ in range(2)]; W["Bt1"] = [S.buf() for _ in range(2)]
    W["t2"] = [S.sb([128, 512], F32) for _ in range(2)]; W["Bt2"] = [S.buf() for _ in range(2)]
    W["sd"] = [S.sb([128, 512], F32) for _ in range(2)]; W["Bsd"] = [S.buf() for _ in range(2)]
    W["ps"] = [S.psum() for _ in range(nps)]; W["Bps"] = [S.buf() for _ in range(nps)]
    W["ps1"], W["Bps1"] = W["ps"][0:2], W["Bps"][0:2]
    W["ps2"], W["Bps2"] = W["ps"][2:4], W["Bps"][2:4]
    return W


def scaled_tables(C, S, W, prm, Bprm, cols):
    P = C.P
    out = []
    for (cg, cgp) in cols:
        Cg = S.sb([128, L], F32); Sg = S.sb([128, L], F32); B = S.buf()
        P.op("dve", lambda e, Cg=Cg, cg=cg: e.tensor_scalar(Cg[:], W["rot"][:, 0, :], prm[:, cg:cg + 1], None, ALU.mult),
             reads=[W["Brot"], Bprm], writes=[B])
        P.op("pool", lambda e, Sg=Sg, cgp=cgp: e.tensor_scalar(Sg[:], W["rot"][:, 1, :], prm[:, cgp:cgp + 1], None, ALU.mult),
             reads=[W["Brot"], Bprm], writes=[B])
        out.append((Cg, Sg, B))
    return out


def gate_and_store(C, S, W, projT, BprojT, yT, ByT, gate_tile, yrow0, yraw_ap, Byraw, scale_ap, Bscale):
    P = C.P
    i = W["n"] % 2
    W["n"] += 1
    xr, Bxr = W["xraw"][i], W["Bxraw"][i]
    sg, Bsg = W["sg"][i], W["Bsg"][i]
    yb, Byb = W["yb"][i], W["Byb"][i]
    P.dma(xr[:], projT[gate_tile * 128:(gate_tile + 1) * 128, :], reads=[BprojT], writes=[Bxr])
    P.op("act", lambda e: e.activation(out=sg[:], in_=xr[:], func=AF.Silu), reads=[Bxr], writes=[Bsg])
    if scale_ap is not None:
        P.op("dve", lambda e: e.scalar_tensor_tensor(yb[:], yraw_ap, scale_ap, sg[:], ALU.mult, ALU.mult),
             reads=[Byraw, Bscale, Bsg], writes=[Byb])
    else:
        P.op("dve", lambda e: e.tensor_tensor(yb[:], yraw_ap, sg[:], ALU.mult), reads=[Byraw, Bsg], writes=[Byb])
    P.dma(yT[yrow0:yrow0 + 128, :], yb[:], reads=[Byb], writes=[ByT], eng="pool")


def phase_p2a(C, lname):
    P, nc = C.P, C.nc
    projT, BprojT = C.dt(lname + "projT", [NFM, L], BF16)
    ptok, Bptok = C.dt(lname + "ptok", [L, NTM], BF16)
    prm_d, Bprm_d = C.dt(lname + "prm", [128, NPRM], F32)
    yT, ByT = C.dt(lname + "yT", [2048, L], BF16)
    stats, Bstats = C.dt(lname + "stats", [2, L], F32)
    with Scope(C, lname + "p2a") as S:
        W = attn_common(C, S, "rotA", 0, 7)
        prm = S.sb([128, NPRM], F32); Bprm = S.dbuf()
        P.dma(prm[:], prm_d[:, :], reads=[Bprm_d], writes=[Bprm])
        (Cq, Sq, Bq), (Ck, Sk, Bk) = scaled_tables(C, S, W, prm, Bprm, [(32, 33), (34, 35)])
        qf = S.sb([128, 4, L], BF16); Bqf = [S.buf() for _ in range(4)]
        kf = S.sb([128, L], BF16); Bkf = S.buf()
        vtok = S.sb([128, 17, 128], BF16); Bv = S.dbuf()
        P.dma(vtok[:, 0:16, :], ptok[0:2048, 1024:1152].rearrange("(s p) c -> p s c", p=128), reads=[Bptok], writes=[Bv])
        P.dma(vtok[0:16, 16, :], ptok[2048:2064, 1024:1152], reads=[Bptok], writes=[Bv])
        for h in range(4):
            qk_prep(C, S, W, projT[h * 128:(h + 1) * 128, :], BprojT, qf[:, h, :], Bqf[h], Cq, Sq, Bq, 0)
        qk_prep(C, S, W, projT[512:640, :], BprojT, kf[:, :], Bkf, Ck, Sk, Bk, 0)
        ps_s, Bps_s = W["ps"][0:2], W["Bps"][0:2]
        ps_o, Bps_o = W["ps"][2:4], W["Bps"][2:4]
        ps_sum, Bps_sum = W["ps"][4:6], W["Bps"][4:6]
        ps_ss, Bps_ss = W["ps"][6], W["Bps"][6]
        pT = [S.sb([128, 512], BF16) for _ in range(3)]; BpT = [S.buf() for _ in range(3)]
        rsb = [S.sb([128, 512], F32) for _ in range(2)]; Brsb = [S.buf() for _ in range(2)]
        osq = [S.sb([128, 512], BF16) for _ in range(2)]; Bosq = [S.buf() for _ in range(2)]
        oraw = S.sb([128, 4, L], F32); Boraw = [S.buf() for _ in range(4)]
        ssrow = S.sb([1, L], F32); Bssrow = S.dbuf()
        scale = 1.0 / math.sqrt(128.0)
        sidx = 0
        it = 0
        for (q0, qw) in TCH:
            for h in range(4):
                o_s = it % 2
                it += 1
                for st, (s0, sr) in enumerate(TT):
                    a = sidx % 2
                    b = sidx % 3
                    sidx += 1
                    P.mm(ps_s[a][:sr, :qw], kf[:, s0:s0 + sr], qf[:, h, q0:q0 + qw], True, True,
                         reads=[Bkf, Bqf[h]], writes=[Bps_s[a]])
                    P.op("act", lambda e, a=a, b=b, sr=sr, qw=qw: e.activation(out=pT[b][:sr, :qw], in_=ps_s[a][:sr, :qw],
                                                                            func=AF.Exp, scale=scale),
                         reads=[Bps_s[a]], writes=[BpT[b]])
                    P.mm(ps_o[o_s][:, :qw], vtok[:sr, st, :], pT[b][:sr, :qw], st == 0, st == 16,
                         reads=[Bv, BpT[b]], writes=[Bps_o[o_s]])
                    P.mm(ps_sum[o_s][:, :qw], W["onesb"][:sr, :], pT[b][:sr, :qw], st == 0, st == 16,
                         reads=[W["Bones"], BpT[b]], writes=[Bps_sum[o_s]])
                P.op("dve", lambda e, o_s=o_s, qw=qw: e.reciprocal(rsb[o_s][:, :qw], ps_sum[o_s][:, :qw]),
                     reads=[Bps_sum[o_s]], writes=[Brsb[o_s]])
                P.op("dve", lambda e, o_s=o_s, qw=qw, h=h, q0=q0: e.tensor_tensor(oraw[:, h, q0:q0 + qw], ps_o[o_s][:, :qw],
                                                                             rsb[o_s][:, :qw], ALU.mult),
                     reads=[Bps_o[o_s], Brsb[o_s]], writes=[Boraw[h]])
                P.op("pool", lambda e, o_s=o_s, qw=qw, h=h, q0=q0: e.tensor_tensor(osq[o_s][:, :qw], oraw[:, h, q0:q0 + qw],
                                                                              oraw[:, h, q0:q0 + qw], ALU.mult),
                     reads=[Boraw[h]], writes=[Bosq[o_s]])
                P.mm(ps_ss[:, :qw], W["onesb"][:], osq[o_s][:, :qw], h == 0, h == 3,
                     reads=[W["Bones"], Bosq[o_s]], writes=[Bps_ss])
            P.op("dve", lambda e, q0=q0, qw=qw: e.tensor_copy(out=ssrow[0:1, q0:q0 + qw], in_=ps_ss[0:1, :qw]),
                 reads=[Bps_ss], writes=[Bssrow])
        P.dma(stats[0:1, :], ssrow[:], reads=[Bssrow], writes=[Bstats])
        W["sg"] = [S.sb([128, L], F32) for _ in range(2)]; W["Bsg"] = [S.buf() for _ in range(2)]
        W["yb"] = [S.sb([128, L], BF16) for _ in range(2)]; W["Byb"] = [S.dbuf() for _ in range(2)]
        for h in range(4):
            gate_and_store(C, S, W, projT, BprojT, yT, ByT, 25 + h, h * 128, oraw[:, h, :], Boraw[h],
                           prm[:, 40 + h:41 + h], Bprm)
        P.finish()


def phase_p2d(C, lname, lam_init):
    P, nc = C.P, C.nc
    projT, BprojT = C.dt(lname + "projT", [NFM, L], BF16)
    ptok, Bptok = C.dt(lname + "ptok", [L, NTM], BF16)
    prm_d, Bprm_d = C.dt(lname + "prm", [128, NPRM], F32)
    dlam_d, Bdlam_d = C.dt(lname + "dlam", [128, 512], F32)
    yT, ByT = C.dt(lname + "yT", [2048, L], BF16)
    with Scope(C, lname + "p2d") as S:
        W = attn_common(C, S, "rotD", 1, 8)
        prm = S.sb([128, NPRM], F32); Bprm = S.dbuf()
        P.dma(prm[:], prm_d[:, :], reads=[Bprm_d], writes=[Bprm])
        (Cq, Sq, Bq), (Ck, Sk, Bk) = scaled_tables(C, S, W, prm, Bprm, [(36, 37), (38, 39)])
        qf = S.sb([128, 4, L], BF16); Bqf = [S.buf() for _ in range(4)]
        kf = S.sb([128, 4, L], BF16); Bkf = [S.buf() for _ in range(4)]
        vtok = S.sb([128, 17, 512], BF16); Bv = S.dbuf()
        P.dma(vtok[:, 0:16, :], ptok[0:2048, 512:1024].rearrange("(s p) c -> p s c", p=128), reads=[Bptok], writes=[Bv])
        P.dma(vtok[0:16, 16, :], ptok[2048:2064, 512:1024], reads=[Bptok], writes=[Bv])
        l4 = S.sb([128, 512], F32); Bl4 = S.dbuf()
        P.dma(l4[:], dlam_d[:, :], reads=[Bdlam_d], writes=[Bl4])
        lt = S.sb([128, 256], F32); Blt = S.buf()
        ls = S.sb([128, 8], F32); Bls = S.buf()
        P.op("dve", lambda e: e.tensor_tensor(lt[:, 0:128], l4[:, 0:128], l4[:, 128:256], ALU.mult), reads=[Bl4], writes=[Blt])
        P.op("dve", lambda e: e.tensor_tensor(lt[:, 128:256], l4[:, 256:384], l4[:, 384:512], ALU.mult), reads=[Bl4], writes=[Blt])
        P.op("dve", lambda e: e.reduce_sum(ls[:, 0:1], lt[:, 0:128], mybir.AxisListType.X), reads=[Blt], writes=[Bls])
        P.op("dve", lambda e: e.reduce_sum(ls[:, 1:2], lt[:, 128:256], mybir.AxisListType.X), reads=[Blt], writes=[Bls])
        P.op("act", lambda e: e.activation(out=ls[:, 2:4], in_=ls[:, 0:2], func=AF.Exp), reads=[Bls], writes=[Bls])
        neglam = S.sb([128, 4], F32); Bnl = S.buf()
        P.op("dve", lambda e: e.tensor_tensor(ls[:, 4:5], ls[:, 3:4], ls[:, 2:3], ALU.subtract), reads=[Bls], writes=[Bls])
        P.op("dve", lambda e: e.tensor_scalar(neglam[:, 0:1], ls[:, 4:5], -float(lam_init), None, ALU.add), reads=[Bls], writes=[Bnl])
        P.op("dve", lambda e: e.tensor_scalar(neglam[:, 1:3], prm[:, 52:54], float(1.0 - lam_init), None, ALU.mult),
             reads=[Bprm], writes=[Bnl])
        for i in range(4):
            qk_prep(C, S, W, projT[(17 + i) * 128:(18 + i) * 128, :], BprojT, qf[:, i, :], Bqf[i], Cq, Sq, Bq, 1)
        for i in range(4):
            qk_prep(C, S, W, projT[(21 + i) * 128:(22 + i) * 128, :], BprojT, kf[:, i, :], Bkf[i], Ck, Sk, Bk, 1)
        ps_s, Bps_s = W["ps"][0:2], W["Bps"][0:2]
        ps_o, Bps_o = W["ps"][2:4], W["Bps"][2:4]
        ps_sum, Bps_sum = W["ps"][4:6], W["Bps"][4:6]
        ps_ss, Bps_ss = W["ps"][6], W["Bps"][6]
        pT = [S.sb([128, 512], BF16) for _ in range(3)]; BpT = [S.buf() for _ in range(3)]
        rsb = [S.sb([128, 512], F32) for _ in range(2)]; Brsb = [S.buf() for _ in range(2)]
        om = [[S.sb([128, 512], F32) for _ in range(2)] for _ in range(2)]
        Bom = [[S.buf() for _ in range(2)] for _ in range(2)]
        ob = [S.sb([128, 512], F32) for _ in range(2)]; Bob = [S.buf() for _ in range(2)]
        osq = [S.sb([128, 512], BF16) for _ in range(2)]; Bosq = [S.buf() for _ in range(2)]
        lnb = S.sb([128, 512], F32); Blnb = S.buf()
        yraw = S.sb([128, 4, L], F32); Byraw = [S.buf() for _ in range(4)]
        scale = 1.0 / math.sqrt(128.0)
        sidx = 0
        it = 0
        for (q0, qw) in TCH:
            for h in range(2):
                for m in range(2):
                    o_s = it % 2
                    it += 1
                    hm = h * 2 + m
                    for st, (s0, sr) in enumerate(TT):
                        a = sidx % 2
                        b = sidx % 3
                        sidx += 1
                        P.mm(ps_s[a][:sr, :qw], kf[:, hm, s0:s0 + sr], qf[:, hm, q0:q0 + qw], True, True,
                             reads=[Bkf[hm], Bqf[hm]], writes=[Bps_s[a]])
                        P.op("act", lambda e, a=a, b=b, sr=sr, qw=qw: e.activation(out=pT[b][:sr, :qw], in_=ps_s[a][:sr, :qw],
                                                                                func=AF.Exp, scale=scale),
                             reads=[Bps_s[a]], writes=[BpT[b]])
                        for half in range(2):
                            P.mm(ps_o[half][:, :qw], vtok[:sr, st, h * 256 + half * 128:h * 256 + (half + 1) * 128],
                                 pT[b][:sr, :qw], st == 0, st == 16, reads=[Bv, BpT[b]], writes=[Bps_o[half]])
                        P.mm(ps_sum[o_s][:, :qw], W["onesb"][:sr, :], pT[b][:sr, :qw], st == 0, st == 16,
                             reads=[W["Bones"], BpT[b]], writes=[Bps_sum[o_s]])
                    P.op("dve", lambda e, o_s=o_s, qw=qw: e.reciprocal(rsb[o_s][:, :qw], ps_sum[o_s][:, :qw]),
                         reads=[Bps_sum[o_s]], writes=[Brsb[o_s]])
                    for half in range(2):
                        P.op("dve", lambda e, o_s=o_s, qw=qw, m=m, half=half: e.tensor_tensor(
                            om[m][half][:, :qw], ps_o[half][:, :qw], rsb[o_s][:, :qw], ALU.mult),
                            reads=[Bps_o[half], Brsb[o_s]], writes=[Bom[m][half]])
                for half in range(2):
                    P.op("dve", lambda e, qw=qw, half=half: e.scalar_tensor_tensor(
                        ob[half][:, :qw], om[1][half][:, :qw], neglam[:, 0:1], om[0][half][:, :qw], ALU.mult, ALU.add),
                        reads=[Bom[0][half], Bom[1][half], Bnl], writes=[Bob[half]])
                    P.op("pool", lambda e, qw=qw, half=half: e.tensor_tensor(osq[half][:, :qw], ob[half][:, :qw],
                                                                          ob[half][:, :qw], ALU.mult),
                         reads=[Bob[half]], writes=[Bosq[half]])
                    P.mm(ps_ss[:, :qw], W["onesb"][:], osq[half][:, :qw], half == 0, half == 1,
                         reads=[W["Bones"], Bosq[half]], writes=[Bps_ss])
                P.op("act", lambda e, qw=qw: e.activation(out=lnb[:, :qw], in_=ps_ss[:, :qw], func=AF.Ln, scale=1.0 / 256,
                                                       bias=W["cst"][:, 0:1]),
                     reads=[Bps_ss, W["Bc"]], writes=[Blnb])
                P.op("act", lambda e, qw=qw: e.activation(out=lnb[:, :qw], in_=lnb[:, :qw], func=AF.Exp, scale=-0.5),
                     reads=[Blnb], writes=[Blnb])
                for half in range(2):
                    P.op("dve", lambda e, qw=qw, q0=q0, h=h, half=half: e.scalar_tensor_tensor(
                        yraw[:, h * 2 + half, q0:q0 + qw], ob[half][:, :qw], neglam[:, 1 + half:2 + half], lnb[:, :qw],
                        ALU.mult, ALU.mult),
                        reads=[Bob[half], Bnl, Blnb], writes=[Byraw[h * 2 + half]])
        W["sg"] = [S.sb([128, L], F32) for _ in range(2)]; W["Bsg"] = [S.buf() for _ in range(2)]
        W["yb"] = [S.sb([128, L], BF16) for _ in range(2)]; W["Byb"] = [S.dbuf() for _ in range(2)]
        for i in range(4):
            gate_and_store(C, S, W, projT, BprojT, yT, ByT, 37 + i, 1536 + i * 128, yraw[:, i, :], Byraw[i], None, None)
        P.finish()


def gate_bufs(C, S, W):
    if "xraw" not in W:
        W["xraw"] = [S.sb([128, L], BF16) for _ in range(2)]; W["Bxraw"] = [S.dbuf() for _ in range(2)]
        W["n"] = 0
    W["sg"] = [S.sb([128, L], F32) for _ in range(2)]; W["Bsg"] = [S.buf() for _ in range(2)]
    W["yb"] = [S.sb([128, L], BF16) for _ in range(2)]; W["Byb"] = [S.dbuf() for _ in range(2)]


def phase_p2c(C, lname):
    P, nc = C.P, C.nc
    projT, BprojT = C.dt(lname + "projT", [NFM, L], BF16)
    ptok, Bptok = C.dt(lname + "ptok", [L, NTM], BF16)
    prm_d, Bprm_d = C.dt(lname + "prm", [128, NPRM], F32)
    cgw_d, Bcgw_d = C.dt(lname + "cgw", [2, 256, 256], F32)
    pool_d, Bpool_d = C.dt("poolm", [2, 5, 128, 6, 512], BF16)
    yT, ByT = C.dt(lname + "yT", [2048, L], BF16)
    with Scope(C, lname + "p2c") as S:
        prm = S.sb([128, NPRM], F32); Bprm = S.dbuf()
        P.dma(prm[:], prm_d[:, :], reads=[Bprm_d], writes=[Bprm])
        cptok = S.sb([128, 17, 512], BF16); Bcp = S.dbuf()
        P.dma(cptok[:, 0:16, :], ptok[0:2048, 0:512].rearrange("(s p) c -> p s c", p=128), reads=[Bptok], writes=[Bcp])
        P.dma(cptok[0:16, 16, :], ptok[2048:2064, 0:512], reads=[Bptok], writes=[Bcp])
        gwst = S.sb([128, 2, 2, 256], F32); Bgwst = S.dbuf()
        for g in range(2):
            P.dma(gwst[:, g, :, :], cgw_d[g].rearrange("(ct p) e -> p ct e", p=128), reads=[Bcgw_d], writes=[Bgwst])
        gwb = S.sb([128, 2, 2, 256], BF16); Bgwb = S.buf()
        P.op("dve", lambda e: e.tensor_copy(out=gwb[:], in_=gwst[:]), reads=[Bgwst], writes=[Bgwb])
        band = [S.sb([128, 6, 512], BF16) for _ in range(2)]; Bband = [S.dbuf() for _ in range(2)]
        dT = [S.sb([128, 2, 512], BF16) for _ in range(2)]; BdT = [[S.buf() for _ in range(2)] for _ in range(2)]
        ycraw = S.sb([128, 4, L], F32); Byc = [S.buf() for _ in range(4)]
        ps_d = [S.psum() for _ in range(2)]; Bps_d = [S.buf() for _ in range(2)]
        ps_y = [S.psum() for _ in range(2)]; Bps_y = [S.buf() for _ in range(2)]
        n1 = n2 = n3 = 0
        for j, (t0, tw) in enumerate(TCH):
            for gl in range(2):
                s = n1 % 2
                n1 += 1
                P.dma(band[s][:], pool_d[gl, j], reads=[Bpool_d], writes=[Bband[s]])
                ks = [k for k in range(6) if 0 <= 4 * j - 1 + k <= 16]
                for ct2 in range(2):
                    ctile = gl * 2 + ct2
                    i = n2 % 2
                    n2 += 1
                    for idx, k in enumerate(ks):
                        st = 4 * j - 1 + k
                        s0, sr = TT[st]
                        P.mm(ps_d[i][:, :tw], cptok[:sr, st, ctile * 128:(ctile + 1) * 128], band[s][:sr, k, :tw],
                             idx == 0, idx == len(ks) - 1, reads=[Bcp, Bband[s]], writes=[Bps_d[i]])
                    P.op("act", lambda e, s=s, i=i, ct2=ct2, tw=tw: e.activation(out=dT[s][:, ct2, :tw], in_=ps_d[i][:, :tw], func=AF.Copy),
                         reads=[Bps_d[i]], writes=[BdT[s][ct2]])
                for et in range(2):
                    i = n3 % 2
                    n3 += 1
                    for ct2 in range(2):
                        P.mm(ps_y[i][:, :tw], gwb[:, gl, ct2, et * 128:(et + 1) * 128], dT[s][:, ct2, :tw], ct2 == 0, ct2 == 1,
                             reads=[Bgwb, BdT[s][ct2]], writes=[Bps_y[i]])
                    P.op("dve", lambda e, i=i, gl=gl, et=et, t0=t0, tw=tw: e.tensor_copy(out=ycraw[:, gl * 2 + et, t0:t0 + tw], in_=ps_y[i][:, :tw]),
                         reads=[Bps_y[i]], writes=[Byc[gl * 2 + et]])
        W = {}
        gate_bufs(C, S, W)
        for i in range(4):
            gate_and_store(C, S, W, projT, BprojT, yT, ByT, 33 + i, 1024 + i * 128, ycraw[:, i, :], Byc[i],
                           prm[:, 48 + i:49 + i], Bprm)
        P.finish()


def phase_p2b(C, lname):
    P, nc = C.P, C.nc
    projT, BprojT = C.dt(lname + "projT", [NFM, L], BF16)
    prm_d, Bprm_d = C.dt(lname + "prm", [128, NPRM], F32)
    fw12_d, Bfw12_d = C.dt(lname + "fw12", [64, 128], F32)
    fw3_d, Bfw3_d = C.dt(lname + "fw3", [64, 2048], F32)
    zfT_d, BzfT_d = C.dt("zfT", [33, L], F32)
    decay_d, Bdecay_d = C.dt("decay", [L, 512], F32)
    Tc_d, BTc_d = C.dt("Tc", [NF, NF], BF16)
    Ts_d, BTs_d = C.dt("Ts", [NF, NF], BF16)
    wf_d, Bwf_d = C.dt("wf", [128, 17], F32)
    ident_d, Bident_d = C.dt("ident", [128, 128], BF16)
    heo_d, Bheo_d = C.dt(lname + "heo", [2, 2, 128, 17, 512], BF16)
    xc_d, Bxc_d = C.dt(lname + "xc", [3, 512, L], BF16)
    ybraw_d, Bybraw_d = C.dt(lname + "ybraw", [512, L], F32)
    yT, ByT = C.dt(lname + "yT", [2048, L], BF16)
    stats, Bstats = C.dt(lname + "stats", [2, L], F32)
    PI = math.pi

    with Scope(C, lname + "p2bF") as S:
        prm = S.sb([128, NPRM], F32); Bprm = S.dbuf()
        P.dma(prm[:], prm_d[:, :], reads=[Bprm_d], writes=[Bprm])
        zf = S.sb([64, L], F32); Bzf = S.dbuf()
        w12 = S.sb([64, 128], F32); Bw12 = S.dbuf()
        w3 = S.sb([64, 2048], F32); Bw3 = S.dbuf()
        P.op("pool", lambda e: e.memset(zf[:], 0.0), writes=[Bzf])
        P.dma(zf[0:33, :], zfT_d[:, :], reads=[BzfT_d], writes=[Bzf])
        P.dma(w12[:], fw12_d[:, :], reads=[Bfw12_d], writes=[Bw12])
        P.dma(w3[:], fw3_d[:, :], reads=[Bfw3_d], writes=[Bw3])
        dec = S.sb([128, 17, 512], F32); Bdec = S.dbuf()
        P.dma(dec[:, 0:16, :], decay_d[0:2048, :].rearrange("(s p) c -> p s c", p=128), reads=[Bdecay_d], writes=[Bdec])
        P.dma(dec[0:16, 16, :], decay_d[2048:2064, :], reads=[Bdecay_d], writes=[Bdec])
        arg = S.sb([64, L], F32); Barg = S.buf()
        tmp = S.sb([64, L], F32); Btmp = S.buf()
        h1 = S.sb([64, L], F32); Bh1 = S.buf()
        h2 = S.sb([64, L], F32); Bh2 = S.buf()
        ps = [S.psum() for _ in range(4)]; Bps = [S.buf() for _ in range(4)]

        def mlp_layer(lhsT_ap, Blhs, src, Bsrc, dst, Bdst, bcol):
            for ci, (t0, tw) in enumerate(TCH):
                i = ci % 2
                P.mm(ps[i][0:64, :tw], lhsT_ap, src[:, t0:t0 + tw], True, True, reads=[Blhs, Bsrc], writes=[Bps[i]])
                P.op("dve", lambda e, i=i, t0=t0, tw=tw: e.tensor_scalar(arg[:, t0:t0 + tw], ps[i][0:64, :tw], prm[0:64, bcol:bcol + 1],
                                                                   prm[0:64, 100:101], ALU.add, ALU.mult),
                     reads=[Bps[i], Bprm], writes=[Barg])
            for _ in range(2):
                P.op("dve", lambda e: e.tensor_scalar(tmp[:], arg[:], PI, 2 * PI, ALU.is_gt, ALU.mult), reads=[Barg], writes=[Btmp])
                P.op("dve", lambda e: e.tensor_tensor(arg[:], arg[:], tmp[:], ALU.subtract), reads=[Btmp], writes=[Barg])
                P.op("dve", lambda e: e.tensor_scalar(tmp[:], arg[:], -PI, 2 * PI, ALU.is_lt, ALU.mult), reads=[Barg], writes=[Btmp])
                P.op("dve", lambda e: e.tensor_tensor(arg[:], arg[:], tmp[:], ALU.add), reads=[Btmp], writes=[Barg])
            P.op("act", lambda e: e.activation(out=dst[:], in_=arg[:], func=AF.Sin), reads=[Barg], writes=[Bdst])

        mlp_layer(w12[:, 0:64], Bw12, zf, Bzf, h1, Bh1, 98)
        mlp_layer(w12[:, 64:128], Bw12, h1, Bh1, h2, Bh2, 99)
        heo = [[S.sb([128, 17, 512], BF16) for _ in range(2)] for _ in range(2)]
        Bheo = [[S.dbuf() for _ in range(2)] for _ in range(2)]
        hf = [S.sb([128, 512], F32) for _ in range(2)]; Bhf = [S.buf() for _ in range(2)]
        hb = [S.sb([128, 512], F32) for _ in range(2)]; Bhb = [S.buf() for _ in range(2)]
        for o in range(2):
            for eo in range(2):
                P.op("pool", lambda e, o=o, eo=eo: e.memset(heo[o][eo][:, 16, :], 0.0), writes=[Bheo[o][eo]])
        n = 0
        for o in range(2):
            for st, (s0, sr) in enumerate(TT):
                i = n % 2
                n += 1
                for dr in range(2):
                    c0 = (o * 2 + dr) * 512
                    P.mm(ps[i * 2 + dr][:sr, :], h2[:, s0:s0 + sr], w3[:, c0:c0 + 512], True, True, reads=[Bh2, Bw3],
                         writes=[Bps[i * 2 + dr]])
                P.op("dve", lambda e, i=i, sr=sr, st=st: e.tensor_tensor(hf[i][:sr, :], ps[i * 2][:sr, :], dec[:sr, st, :], ALU.mult),
                     reads=[Bps[i * 2], Bdec], writes=[Bhf[i]])
                P.op("dve", lambda e, i=i, sr=sr, st=st: e.tensor_tensor(hb[i][:sr, :], ps[i * 2 + 1][:sr, :], dec[:sr, st, :], ALU.mult),
                     reads=[Bps[i * 2 + 1], Bdec], writes=[Bhb[i]])
                if st == 0:
                    P.op("dve", lambda e, i=i: e.memset(hb[i][0:1, :], 0.0), writes=[Bhb[i]])
                P.op("pool", lambda e, i=i, sr=sr, st=st, o=o: e.tensor_tensor(heo[o][0][:sr, st, :], hf[i][:sr, :], hb[i][:sr, :], ALU.add),
                     reads=[Bhf[i], Bhb[i]], writes=[Bheo[o][0]])
                P.op("pool", lambda e, i=i, sr=sr, st=st, o=o: e.tensor_tensor(heo[o][1][:sr, st, :], hf[i][:sr, :], hb[i][:sr, :], ALU.subtract),
                     reads=[Bhf[i], Bhb[i]], writes=[Bheo[o][1]])
            for eo in range(2):
                P.dma(heo_d[o, eo], heo[o][eo][:], reads=[Bheo[o][eo]], writes=[Bheo_d])
        P.finish()

    with Scope(C, lname + "p2bX") as S:
        prm = S.sb([128, NPRM], F32); Bprm = S.dbuf()
        P.dma(prm[:], prm_d[:, :], reads=[Bprm_d], writes=[Bprm])
        xr = [S.sb([128, L], BF16) for _ in range(2)]; Bxr = [S.dbuf() for _ in range(2)]
        uu = [S.sb([128, L], F32) for _ in range(2)]; Buu = [S.buf() for _ in range(2)]
        ub = [S.sb([128, L], BF16) for _ in range(2)]; Bub = [S.dbuf() for _ in range(2)]
        n = 0
        for sec in range(3):
            for ct in range(4):
                i = n % 2
                n += 1
                tile = 5 + sec * 4 + ct
                cw = [54 + tap * 12 + sec * 4 + ct for tap in range(3)]
                P.dma(xr[i][:], projT[tile * 128:(tile + 1) * 128, :], reads=[BprojT], writes=[Bxr[i]])
                P.op("dve", lambda e, i=i, cw=cw: e.tensor_scalar(uu[i][:], xr[i][:], prm[:, cw[1]:cw[1] + 1], None, ALU.mult),
                     reads=[Bxr[i], Bprm], writes=[Buu[i]])
                P.op("dve", lambda e, i=i, cw=cw: e.scalar_tensor_tensor(uu[i][:, 1:L], xr[i][:, 0:L - 1], prm[:, cw[0]:cw[0] + 1],
                                                                      uu[i][:, 1:L], ALU.mult, ALU.add),
                     reads=[Bxr[i], Bprm], writes=[Buu[i]])
                P.op("dve", lambda e, i=i, cw=cw: e.scalar_tensor_tensor(uu[i][:, 0:L - 1], xr[i][:, 1:L], prm[:, cw[2]:cw[2] + 1],
                                                                      uu[i][:, 0:L - 1], ALU.mult, ALU.add),
                     reads=[Bxr[i], Bprm], writes=[Buu[i]])
                P.op("act", lambda e, i=i: e.activation(out=ub[i][:], in_=uu[i][:], func=AF.Copy), reads=[Buu[i]], writes=[Bub[i]])
                P.dma(xc_d[sec, ct * 128:(ct + 1) * 128, :], ub[i][:], reads=[Bub[i]], writes=[Bxc_d], eng="pool")
        P.finish()

    with Scope(C, lname + "p2bO") as S:
        prm = S.sb([128, NPRM], F32); Bprm = S.dbuf()
        P.dma(prm[:], prm_d[:, :], reads=[Bprm_d], writes=[Bprm])
        wf = S.sb([128, 17], F32); Bwf = S.dbuf()
        P.dma(wf[:], wf_d[:, :], reads=[Bwf_d], writes=[Bwf])
        ident = S.sb([128, 128], BF16); Bident = S.dbuf()
        P.dma(ident[:], ident_d[:, :], reads=[Bident_d], writes=[Bident])
        onesb = S.sb([128, 128], BF16); Bones = S.buf()
        P.op("pool", lambda e: e.memset(onesb[:], 1.0), writes=[Bones])
        z = S.sb([128, 17, 512], BF16); Bz = S.buf()
        slots = [S.sb([128, 17, 512], BF16) for _ in range(4)]; Bsl = [S.dbuf() for _ in range(4)]
        Y = S.sb([128, 17, 2, 512], BF16); BY = [S.buf() for _ in range(17)]
        vA = S.sb([128, 4, L], BF16); BvA = S.dbuf()
        vB = S.sb([128, 4, L], BF16); BvB = S.buf()
        hcs = [[S.sb([128, 512], F32) for _ in range(2)] for _ in range(2)]
        Bhcs = [[S.buf() for _ in range(2)] for _ in range(2)]
        ta = [S.sb([128, 512], F32) for _ in range(4)]; Bta = [S.buf() for _ in range(4)]
        xo = [S.sb([128, 512], BF16) for _ in range(2)]; Bxo = [S.dbuf() for _ in range(2)]
        tq = [S.sb([128, 512], F32) for _ in range(2)]; Btq = [S.buf() for _ in range(2)]
        yo = [S.sb([128, 512], F32) for _ in range(2)]; Byo = [S.dbuf() for _ in range(2)]
        sqb = [S.sb([128, 512], BF16) for _ in range(2)]; Bsqb = [S.buf() for _ in range(2)]
        ssb = [S.sb([1, 512], F32) for _ in range(2)]; Bssb = [S.dbuf() for _ in range(2)]
        ps = [S.psum() for _ in range(8)]; Bps = [S.buf() for _ in range(8)]
        P.dma(vA[:], xc_d[0].rearrange("(ct p) t -> p ct t", p=128), reads=[Bxc_d], writes=[BvA])

        def make_z(src, Bsrc):
            n = 0
            for st, (s0, sr) in enumerate(TT):
                i = n % 2
                n += 1
                for ct in range(4):
                    P.mm(ps[i][:sr, ct * 128:(ct + 1) * 128], src[:, ct, s0:s0 + sr], ident[:], True, True,
                         reads=[Bsrc, Bident], writes=[Bps[i]])
                if st % 2 == 0:
                    P.op("act", lambda e, i=i, sr=sr, st=st: e.activation(out=z[:sr, st, :], in_=ps[i][:sr, :], func=AF.Copy),
                         reads=[Bps[i]], writes=[Bz])
                else:
                    P.op("dve", lambda e, i=i, sr=sr, st=st: e.tensor_copy(out=z[:sr, st, :], in_=ps[i][:sr, :]),
                         reads=[Bps[i]], writes=[Bz])

        cnt = {"f": 0, "i": 0, "x": 0}
        for o in range(2):
            vcur, Bvcur = (vA, BvA) if o == 0 else (vB, BvB)
            make_z(vcur, Bvcur)
            P.dma(slots[0][:], heo_d[o, 0], reads=[Bheo_d], writes=[Bsl[0]])
            P.dma(slots[1][:], heo_d[o, 1], reads=[Bheo_d], writes=[Bsl[1]])
            for (ft0, nft) in FG:
                P.dma(slots[2][:, :, :nft * 128], Tc_d[:, ft0 * 128:(ft0 + nft) * 128].rearrange("(st p) f -> p st f", p=128),
                      reads=[BTc_d], writes=[Bsl[2]])
                P.dma(slots[3][:, :, :nft * 128], Ts_d[:, ft0 * 128:(ft0 + nft) * 128].rearrange("(st p) f -> p st f", p=128),
                      reads=[BTs_d], writes=[Bsl[3]])
                for fi in range(nft):
                    ft = ft0 + fi
                    b = (cnt["f"] % 2) * 4
                    k = cnt["f"] % 2
                    cnt["f"] += 1
                    Zc, Zs, Hc, Hs = ps[b], ps[b + 1], ps[b + 2], ps[b + 3]
                    for st, (s0, sr) in enumerate(TT):
                        lc = slots[2][:sr, st, fi * 128:(fi + 1) * 128]
                        lsn = slots[3][:sr, st, fi * 128:(fi + 1) * 128]
                        P.mm(Zc[:, :], lc, z[:sr, st, :], st == 0, st == 16, reads=[Bsl[2], Bz], writes=[Bps[b]])
                        P.mm(Hc[:, :], lc, slots[0][:sr, st, :], st == 0, st == 16, reads=[Bsl[2], Bsl[0]], writes=[Bps[b + 2]])
                        P.mm(Zs[:, :], lsn, z[:sr, st, :], st == 0, st == 16, reads=[Bsl[3], Bz], writes=[Bps[b + 1]])
                        P.mm(Hs[:, :], lsn, slots[1][:sr, st, :], st == 0, st == 16, reads=[Bsl[3], Bsl[1]], writes=[Bps[b + 3]])
                    P.op("act", lambda e, k=k, ft=ft, Hc=Hc: e.activation(out=hcs[k][0][:], in_=Hc[:, :], func=AF.Copy, scale=wf[:, ft:ft + 1]),
                         reads=[Bps[b + 2], Bwf], writes=[Bhcs[k][0]])
                    P.op("act", lambda e, k=k, ft=ft, Hs=Hs: e.activation(out=hcs[k][1][:], in_=Hs[:, :], func=AF.Copy, scale=wf[:, ft:ft + 1]),
                         reads=[Bps[b + 3], Bwf], writes=[Bhcs[k][1]])
                    P.op("dve", lambda e, k=k, Zc=Zc: e.tensor_tensor(ta[0][:], Zc[:, :], hcs[k][0][:], ALU.mult),
                         reads=[Bps[b], Bhcs[k][0]], writes=[Bta[0]])
                    P.op("dve", lambda e, k=k, Zs=Zs: e.tensor_tensor(ta[1][:], Zs[:, :], hcs[k][1][:], ALU.mult),
                         reads=[Bps[b + 1], Bhcs[k][1]], writes=[Bta[1]])
                    P.op("pool", lambda e, ft=ft: e.tensor_tensor(Y[:, ft, 0, :], ta[0][:], ta[1][:], ALU.subtract),
                         reads=[Bta[0], Bta[1]], writes=[BY[ft]])
                    P.op("dve", lambda e, k=k, Zc=Zc: e.tensor_tensor(ta[2][:], Zc[:, :], hcs[k][1][:], ALU.mult),
                         reads=[Bps[b], Bhcs[k][1]], writes=[Bta[2]])
                    P.op("dve", lambda e, k=k, Zs=Zs: e.tensor_tensor(ta[3][:], Zs[:, :], hcs[k][0][:], ALU.mult),
                         reads=[Bps[b + 1], Bhcs[k][0]], writes=[Bta[3]])
                    P.op("pool", lambda e, ft=ft: e.tensor_tensor(Y[:, ft, 1, :], ta[2][:], ta[3][:], ALU.add),
                         reads=[Bta[2], Bta[3]], writes=[BY[ft]])
            for tj, (t0, tw) in enumerate(TCH):
                a = (cnt["i"] % 2) * 2
                cnt["i"] += 1
                P.dma(slots[a][:, :, :tw], Tc_d[:, t0:t0 + tw].rearrange("(ft p) t -> p ft t", p=128), reads=[BTc_d], writes=[Bsl[a]])
                P.dma(slots[a + 1][:, :, :tw], Ts_d[:, t0:t0 + tw].rearrange("(ft p) t -> p ft t", p=128), reads=[BTs_d], writes=[Bsl[a + 1]])
                for ct in range(4):
                    i = cnt["x"] % 2
                    cnt["x"] += 1
                    pacc, Bpacc = ps[i], Bps[i]
                    P.dma(xo[i][:, :tw], xc_d[1 + o, ct * 128:(ct + 1) * 128, t0:t0 + tw], reads=[Bxc_d], writes=[Bxo[i]], eng="pool")
                    for ft in range(17):
                        P.mm(pacc[:, :tw], Y[:, ft, 0, ct * 128:(ct + 1) * 128], slots[a][:, ft, :tw], ft == 0, False,
                             reads=[BY[ft], Bsl[a]], writes=[Bpacc])
                        P.mm(pacc[:, :tw], Y[:, ft, 1, ct * 128:(ct + 1) * 128], slots[a + 1][:, ft, :tw], False, ft == 16,
                             reads=[BY[ft], Bsl[a + 1]], writes=[Bpacc])
                    sk = 90 + o * 4 + ct
                    P.op("dve", lambda e, i=i, ct=ct, t0=t0, tw=tw, sk=sk, vcur=vcur, pacc=pacc: e.scalar_tensor_tensor(
                        tq[i][:, :tw], vcur[:, ct, t0:t0 + tw], prm[:, sk:sk + 1], pacc[:, :tw], ALU.mult, ALU.add),
                        reads=[Bvcur, Bprm, Bpacc], writes=[Btq[i]])
                    if o == 0:
                        P.op("pool", lambda e, i=i, ct=ct, t0=t0, tw=tw: e.tensor_tensor(vB[:, ct, t0:t0 + tw], tq[i][:, :tw], xo[i][:, :tw], ALU.mult),
                             reads=[Btq[i], Bxo[i]], writes=[BvB])
                    else:
                        P.op("pool", lambda e, i=i, tw=tw: e.tensor_tensor(yo[i][:, :tw], tq[i][:, :tw], xo[i][:, :tw], ALU.mult),
                             reads=[Btq[i], Bxo[i]], writes=[Byo[i]])
                        P.dma(ybraw_d[ct * 128:(ct + 1) * 128, t0:t0 + tw], yo[i][:, :tw], reads=[Byo[i]], writes=[Bybraw_d], eng="pool")
                        P.op("pool", lambda e, i=i, tw=tw: e.tensor_tensor(sqb[i][:, :tw], yo[i][:, :tw], yo[i][:, :tw], ALU.mult),
                             reads=[Byo[i]], writes=[Bsqb[i]])
                        P.mm(ps[4][:, :tw], onesb[:], sqb[i][:, :tw], ct == 0, ct == 3, reads=[Bones, Bsqb[i]], writes=[Bps[4]])
                if o == 1:
                    q = tj % 2
                    P.op("dve", lambda e, q=q, tw=tw: e.tensor_copy(out=ssb[q][0:1, :tw], in_=ps[4][0:1, :tw]),
                         reads=[Bps[4]], writes=[Bssb[q]])
                    P.dma(stats[1:2, t0:t0 + tw], ssb[q][0:1, :tw], reads=[Bssb[q]], writes=[Bstats], eng="pool")
        P.finish()

    with Scope(C, lname + "p2bG") as S:
        prm = S.sb([128, NPRM], F32); Bprm = S.dbuf()
        P.dma(prm[:], prm_d[:, :], reads=[Bprm_d], writes=[Bprm])
        W = {}
        gate_bufs(C, S, W)
        yr = [S.sb([128, L], F32) for _ in range(2)]; Byr = [S.dbuf() for _ in range(2)]
        for ct in range(4):
            i = ct % 2
            P.dma(yr[i][:], ybraw_d[ct * 128:(ct + 1) * 128, :], reads=[Bybraw_d], writes=[Byr[i]])
            gate_and_store(C, S, W, projT, BprojT, yT, ByT, 29 + ct, 512 + ct * 128, yr[i][:], Byr[i],
                           prm[:, 44 + ct:45 + ct], Bprm)
        P.finish()


def phase_p3(C, lname, nname, last):
    P, nc = C.P, C.nc
    ygh, Bygh = C.dt(lname + "y_gh", [2, 2048, HALF], BF16)
    sgh, Bsgh = C.dt(lname + "stats_gh", [2, 2, HALF], F32)
    hT, BhT = C.dt(lname + "hT", [DM, HALF], F32)
    wo, Bwo = C.dt(lname + "w_out_p", [DM, DM], F32)
    hTn, BhTn = C.dt(nname + "hT", [DM, HALF], F32)
    if not last:
        uTn, BuTn = C.dt(nname + "uT_half", [DM, HALF], BF16)
    with Scope(C, lname + "p3") as S:
        cst, Bc = load_consts(C, S, None)
        ones32 = S.sb([32, 128], F32); Bo32 = S.buf()
        P.op("pool", lambda e: e.memset(ones32[:], 1.0), writes=[Bo32])
        onesf = S.sb([128, 128], F32); Bonesf = S.buf()
        P.op("pool", lambda e: e.memset(onesf[:], 1.0), writes=[Bonesf])
        ps = [S.psum() for _ in range(8)]; Bps = [S.buf() for _ in range(8)]
        ss2 = [S.sb([32, HALF], F32) for _ in range(2)]; Bss2 = [S.dbuf() for _ in range(2)]
        rs = [S.sb([128, HALF], F32) for _ in range(2)]; Brs = [S.buf() for _ in range(2)]
        for w in range(2):
            P.op("pool", lambda e, w=w: e.memset(ss2[w][:], 0.0), writes=[Bss2[w]])
            P.dma(ss2[w][0:2, :], sgh[:, w, :], reads=[Bsgh], writes=[Bss2[w]])
            for ci, (c0, cw) in enumerate(HCH):
                P.mm(ps[ci][:, :cw], ones32[:, :], ss2[w][:, c0:c0 + cw], True, True, reads=[Bo32, Bss2[w]], writes=[Bps[ci]])
                P.op("act", lambda e, w=w, ci=ci, c0=c0, cw=cw: e.activation(out=rs[w][:, c0:c0 + cw], in_=ps[ci][:, :cw], func=AF.Sqrt,
                                                                         scale=1.0 / 1024, bias=cst[:, 0:1]),
                     reads=[Bps[ci], Bc], writes=[Brs[w]])
            P.op("dve", lambda e, w=w: e.reciprocal(rs[w][:], rs[w][:]), reads=[Brs[w]], writes=[Brs[w]])
        y = S.sb([128, 32, HALF], BF16); By = [S.dbuf() for _ in range(8)]
        for rk in range(2):
            for q in range(4):
                P.dma(y[:, rk * 16 + q * 4:rk * 16 + q * 4 + 4, :],
                      ygh[rk, q * 512:(q + 1) * 512, :].rearrange("(k p) t -> p k t", p=128), reads=[Bygh], writes=[By[rk * 4 + q]])
                if q < 2:
                    for k in range(4):
                        ct = rk * 16 + q * 4 + k
                        eng = "dve" if k % 2 == 0 else "pool"
                        P.op(eng, lambda e, ct=ct, q=q: e.tensor_tensor(y[:, ct, :], y[:, ct, :], rs[q][:], ALU.mult),
                             reads=[Brs[q]], writes=[By[rk * 4 + q]])
        wst = [S.sb([128, 8, 256], F32) for _ in range(2)]; Bwst = [S.dbuf() for _ in range(2)]
        wb = [S.sb([128, 32, 256], BF16) for _ in range(2)]
        Bwb = [[S.buf(multi=True) for _ in range(4)] for _ in range(2)]
        hres = [S.sb([128, HALF], F32) for _ in range(2)]; Bhres = [S.dbuf() for _ in range(2)]
        hn = [S.sb([128, HALF], F32) for _ in range(2)]; Bhn = [S.dbuf() for _ in range(2)]
        if not last:
            sq = S.sb([128, HALF], F32); Bsq = S.buf()
            acc = S.sb([128, HALF], F32); Bacc = S.buf()
            P.op("pool", lambda e: e.memset(acc[:], 0.0), writes=[Bacc])
        cnt = {"piece": 0, "cast": 0, "nt": 0}

        def load_group(g):
            gs = g % 2
            for pc in range(4):
                s = cnt["piece"] % 2
                cnt["piece"] += 1
                P.dma(wst[s][:], wo[pc * 1024:(pc + 1) * 1024, g * 256:(g + 1) * 256].rearrange("(k p) n -> p k n", p=128),
                      reads=[Bwo], writes=[Bwst[s]])
                if cnt["cast"] % 2 == 0:
                    P.op("act", lambda e, s=s, gs=gs, pc=pc: e.activation(out=wb[gs][:, pc * 8:(pc + 1) * 8, :], in_=wst[s][:], func=AF.Copy),
                         reads=[Bwst[s]], writes=[Bwb[gs][pc]])
                else:
                    P.op("dve", lambda e, s=s, gs=gs, pc=pc: e.tensor_copy(out=wb[gs][:, pc * 8:(pc + 1) * 8, :], in_=wst[s][:]),
                         reads=[Bwst[s]], writes=[Bwb[gs][pc]])
                cnt["cast"] += 1

        load_group(0)
        for g in range(16):
            if g + 1 < 16:
                load_group(g + 1)
            gs = g % 2
            for nt in range(2):
                n0 = g * 256 + nt * 128
                ss = cnt["nt"] % 2
                cnt["nt"] += 1
                P.dma(hres[ss][:], hT[n0:n0 + 128, :], reads=[BhT], writes=[Bhres[ss]])
                for ct in range(32):
                    for ci, (t0, tw) in enumerate(HCH):
                        P.mm(ps[ss * 3 + ci][:, :tw], wb[gs][:, ct, nt * 128:(nt + 1) * 128], y[:, ct, t0:t0 + tw],
                             ct == 0, ct == 31, reads=[Bwb[gs][ct // 8], By[ct // 4]], writes=[Bps[ss * 3 + ci]])
                for ci, (t0, tw) in enumerate(HCH):
                    P.op("dve", lambda e, ss=ss, ci=ci, t0=t0, tw=tw: e.tensor_tensor(hn[ss][:, t0:t0 + tw], ps[ss * 3 + ci][:, :tw],
                                                                                   hres[ss][:, t0:t0 + tw], ALU.add),
                         reads=[Bps[ss * 3 + ci], Bhres[ss]], writes=[Bhn[ss]])
                P.dma(hTn[n0:n0 + 128, :], hn[ss][:], reads=[Bhn[ss]], writes=[BhTn], eng="pool")
                if not last:
                    P.op("act", lambda e, ss=ss: e.activation(out=sq[:], in_=hn[ss][:], func=AF.Square), reads=[Bhn[ss]], writes=[Bsq])
                    P.op("pool", lambda e: e.tensor_tensor(acc[:], acc[:], sq[:], ALU.add), reads=[Bsq], writes=[Bacc])
        if not last:
            rstd = S.sb([128, HALF], F32); Brstd = S.buf()
            for ci, (c0, cw) in enumerate(HCH):
                P.mm(ps[6][:, :cw], onesf[:], acc[:, c0:c0 + cw], True, True, reads=[Bonesf, Bacc], writes=[Bps[6]])
                P.op("act", lambda e, c0=c0, cw=cw: e.activation(out=rstd[:, c0:c0 + cw], in_=ps[6][:, :cw], func=AF.Sqrt,
                                                             scale=1.0 / DM, bias=cst[:, 0:1]),
                     reads=[Bps[6], Bc], writes=[Brstd])
            P.op("dve", lambda e: e.reciprocal(rstd[:], rstd[:]), reads=[Brstd], writes=[Brstd])
            ub = [S.sb([128, HALF], BF16) for _ in range(2)]; Bub = [S.dbuf() for _ in range(2)]
            for kt in range(32):
                s = kt % 2
                P.dma(hres[s][:], hTn[kt * 128:(kt + 1) * 128, :], reads=[BhTn], writes=[Bhres[s]])
                P.op("dve", lambda e, s=s: e.tensor_tensor(ub[s][:], hres[s][:], rstd[:], ALU.mult),
                     reads=[Bhres[s], Brstd], writes=[Bub[s]])
                P.dma(uTn[kt * 128:(kt + 1) * 128, :], ub[s][:], reads=[Bub[s]], writes=[BuTn], eng="pool")
        P.finish()


NCORES = 8
_PROG_CACHE = {}
DEBUG = None


def _prog(key, phases, roles):
    if key not in _PROG_CACHE:
        _PROG_CACHE[key] = build_program(phases, roles)
    return _PROG_CACHE[key]


def _lam_init(l):
    return 0.8 - 0.6 * math.exp(-0.3 * l)


def mixer_inputs(inp, l, r, cst, pre):
    fw12 = np.zeros((64, 128), np.float32)
    fw12[0:33, 0:64] = inp["b_filt_w1"][l]
    fw12[:, 64:128] = inp["b_filt_w2"][l]
    w3 = inp["b_filt_w3"][l].reshape(64, 4, 1024)[:, :, r * 512:(r + 1) * 512].reshape(64, 2048)
    return {
        pre + "w_fm": np.ascontiguousarray(inp["w_in"][l][:, fm_cols(r)]),
        pre + "w_tm": np.ascontiguousarray(inp["w_in"][l][:, tm_cols(r)]),
        pre + "prm": pack_layer_params(inp, l, r),
        pre + "dlam": np.ascontiguousarray(np.broadcast_to(inp["d_lambda"][l].reshape(1, 512), (128, 512))),
        pre + "cgw": np.ascontiguousarray(inp["c_group_w"][l][2 * r:2 * r + 2]),
        pre + "fw12": fw12,
        pre + "fw3": np.ascontiguousarray(w3),
        "rotA": cst["rotA"], "rotD": cst["rotD"], "Rm": cst["Rm"], "poolm": cst["pool"][r],
        "zfT": cst["zfT"], "decay": cst["decay"][r], "Tc": cst["Tc"], "Ts": cst["Ts"], "wf": cst["wf"],
        "ident": cst["ident"],
    }


MIXER_IN = ["w_fm", "w_tm", "prm", "dlam", "cgw", "fw12", "fw3"]
CONST_IN = ["rotA", "rotD", "Rm", "poolm", "zfT", "decay", "Tc", "Ts", "wf", "ident"]


def kernel(**inputs):
    inp = {k: np.asarray(v) for k, v in inputs.items()}
    cst = host_consts()
    cores = list(range(NCORES))
    x = inp["x"]
    meta = inp["meta_tokens"]
    hT = []
    for c in cores:
        b, r = c // 2, c % 2
        h0 = np.concatenate([meta, x[b]], axis=0)
        hT.append(np.ascontiguousarray(h0[r * HALF:(r + 1) * HALF].T))
    nc = _prog("p0", [lambda C: phase_p0(C, "l0_")], {"l0_hT": "in", "l0_uT_half": "out"})
    res = run_bass_kernel_spmd(nc, [{"l0_hT": hT[c]} for c in cores], core_ids=cores)
    uT_half = [res.results[c]["l0_uT_half"] for c in cores]
    for l in range(DEPTH):
        pre = "l%d_" % l
        nxt = "l%d_" % (l + 1)
        last = (l == DEPTH - 1)
        lam = _lam_init(l)
        roles = {pre + "uT_g": "in", pre + "yT": "out", pre + "stats": "out"}
        for n in MIXER_IN:
            roles[pre + n] = "in"
        for n in CONST_IN:
            roles[n] = "in"
        phases = [lambda C, pre=pre: phase_p1(C, pre), lambda C, pre=pre: phase_p2a(C, pre),
                  lambda C, pre=pre, lam=lam: phase_p2d(C, pre, lam), lambda C, pre=pre: phase_p2c(C, pre),
                  lambda C, pre=pre: phase_p2b(C, pre)]
        nc = _prog("mix%d" % l, phases, roles)
        shared = [mixer_inputs(inp, l, r, cst, pre) for r in range(2)]
        in_maps = []
        for c in cores:
            b, r = c // 2, c % 2
            m = dict(shared[r])
            m[pre + "uT_g"] = np.stack([uT_half[2 * b], uT_half[2 * b + 1]])
            in_maps.append(m)
        res = run_bass_kernel_spmd(nc, in_maps, core_ids=cores)
        yT = [res.results[c][pre + "yT"] for c in cores]
        stats = [res.results[c][pre + "stats"] for c in cores]
        if DEBUG is not None:
            DEBUG[pre + "yT"] = yT
            DEBUG[pre + "stats"] = stats
        roles = {pre + "y_gh": "in", pre + "stats_gh": "in", pre + "hT": "in", pre + "w_out_p": "in", nxt + "hT": "out"}
        if not last:
            roles[nxt + "uT_half"] = "out"
        nc = _prog("p3_%d" % l, [lambda C, pre=pre, nxt=nxt, last=last: phase_p3(C, pre, nxt, last)], roles)
        w_out_p = np.ascontiguousarray(inp["w_out"][l][np.concatenate([y_rows(0), y_rows(1)]), :])
        in_maps = []
        for c in cores:
            b, r = c // 2, c % 2
            sl = slice(r * HALF, (r + 1) * HALF)
            in_maps.append({
                pre + "y_gh": np.ascontiguousarray(np.stack([yT[2 * b][:, sl], yT[2 * b + 1][:, sl]])),
                pre + "stats_gh": np.ascontiguousarray(np.stack([stats[2 * b][:, sl], stats[2 * b + 1][:, sl]])),
                pre + "hT": hT[c],
                pre + "w_out_p": w_out_p,
            })
        res = run_bass_kernel_spmd(nc, in_maps, core_ids=cores)
        hT = [res.results[c][nxt + "hT"] for c in cores]
        if not last:
            uT_half = [res.results[c][nxt + "uT_half"] for c in cores]
        if DEBUG is not None:
            DEBUG[nxt + "hT"] = hT
    out = np.empty((4, SEQ, DM), np.float32)
    for b in range(4):
        full = np.concatenate([hT[2 * b].T, hT[2 * b + 1].T], axis=0)
        out[b] = full[NMETA:]
    return out
```
